# Optimizing a Trainium2 kernel written in Bass

```python
import math
import jax, jax.numpy as jnp
from jax import lax
import numpy as np

D_MODEL = 2048
BATCH = 8
SEQ = 2048
DEPTH = 1

CHUNK = 64
Q_BLOCK = 128
EPS = 1e-6
MIX_WIDTH = D_MODEL
DA_HEADS = 8
DA_V_DIM = (MIX_WIDTH // 2) // DA_HEADS
DA_QK_DIM = DA_V_DIM // 2
DA_WIDTH = DA_HEADS * DA_V_DIM
DA_QK_WIDTH = DA_HEADS * 2 * DA_QK_DIM
ML_HEADS = 4
ML_HEAD_DIM = (MIX_WIDTH // 2) // ML_HEADS
ML_WIDTH = ML_HEADS * ML_HEAD_DIM
CONV_K = 4
D_FF = 4 * D_MODEL
PLE_DIM = 256
SPLIT_SIZES = (DA_QK_WIDTH, DA_QK_WIDTH, DA_WIDTH,
               ML_WIDTH, ML_WIDTH, ML_WIDTH, ML_WIDTH, ML_HEADS, ML_HEADS,
               D_MODEL, D_MODEL)
IN_COLS = sum(SPLIT_SIZES)
NEG_INF = -1e30

kernel_name = 'hybrid_diffattn_mlstm_block'


def rmsnorm(x, g):
    xf = x.astype(jnp.float32)
    y = xf * lax.rsqrt(jnp.mean(xf * xf, axis=-1, keepdims=True) + EPS)
    return (y * g.astype(jnp.float32)).astype(x.dtype)


def diff_attention(q, k, v, lam_q1, lam_k1, lam_q2, lam_k2, sub_g, lambda_init):
    bsz, seq = q.shape[0], q.shape[1]
    qf = q.astype(jnp.float32).reshape(bsz, seq, DA_HEADS, 2, DA_QK_DIM) * (DA_QK_DIM ** -0.5)
    kf = k.astype(jnp.float32).reshape(bsz, seq, DA_HEADS, 2, DA_QK_DIM)
    vf = v.astype(jnp.float32).reshape(bsz, seq, DA_HEADS, DA_V_DIM)
    f32 = jnp.float32
    lam = (jnp.exp(jnp.sum(lam_q1.astype(f32) * lam_k1.astype(f32)))
           - jnp.exp(jnp.sum(lam_q2.astype(f32) * lam_k2.astype(f32))) + lambda_init)
    outs = []
    for blk in range(seq // Q_BLOCK):
        q0 = blk * Q_BLOCK
        k_end = q0 + Q_BLOCK
        s = jnp.einsum('bqhmd,bkhmd->bhmqk', qf[:, q0:k_end], kf[:, :k_end])
        mask = (np.arange(k_end)[None, :] // CHUNK) <= (np.arange(q0, k_end)[:, None] // CHUNK)
        a = jax.nn.softmax(jnp.where(mask, s, NEG_INF), axis=-1)
        w = a[:, :, 0] - lam * a[:, :, 1]
        outs.append(jnp.einsum('bhqk,bkhd->bqhd', w, vf[:, :k_end]))
    o = jnp.concatenate(outs, axis=1)
    o = rmsnorm(o, sub_g) * (1.0 - lambda_init)
    return o.reshape(bsz, seq, DA_WIDTH).astype(q.dtype)


def causal_conv_silu(x, w, b):
    seq = x.shape[1]
    xp = jnp.pad(x, ((0, 0), (CONV_K - 1, 0), (0, 0)))
    y = b + sum(xp[:, j:j + seq] * w[j] for j in range(CONV_K))
    return jax.nn.silu(y)


def _to_chunks(t):
    bsz, seq, h = t.shape[0], t.shape[1], t.shape[2]
    t = jnp.swapaxes(t, 1, 2).reshape(bsz, h, seq // CHUNK, CHUNK, *t.shape[3:])
    return jnp.moveaxis(t, 2, 0)


def mlstm_chunkwise(q, k, v, log_i, log_f):
    bsz, seq, nh, d = q.shape
    qc = _to_chunks(q * (d ** -0.5))
    kc = _to_chunks(k)
    vc = _to_chunks(v)
    ic = _to_chunks(log_i)
    bc = jnp.cumsum(_to_chunks(log_f), axis=-1)
    causal = np.tril(np.ones((CHUNK, CHUNK), dtype=bool))

    def step(carry, xs):
        c_prev, n_prev, m_prev = carry
        q_c, k_c, v_c, i_c, b_c = xs
        dlog = b_c[..., :, None] - b_c[..., None, :] + i_c[..., None, :]
        dlog = jnp.where(causal, dlog, NEG_INF)
        inter_log = b_c + m_prev[..., None]
        m_t = jnp.maximum(inter_log, jnp.max(dlog, axis=-1))
        dmat = jnp.exp(dlog - m_t[..., None])
        inter_w = jnp.exp(inter_log - m_t)
        s = jnp.einsum('bhld,bhsd->bhls', q_c, k_c) * dmat
        num = (jnp.einsum('bhls,bhsd->bhld', s, v_c)
               + inter_w[..., None] * jnp.einsum('bhld,bhde->bhle', q_c, c_prev))
        den = jnp.sum(s, axis=-1) + inter_w * jnp.einsum('bhld,bhd->bhl', q_c, n_prev)
        h_out = num / jnp.maximum(jnp.abs(den), jnp.exp(-m_t))[..., None]
        b_last = b_c[..., -1]
        upd_log = b_last[..., None] - b_c + i_c
        m_new = jnp.maximum(b_last + m_prev, jnp.max(upd_log, axis=-1))
        w_s = jnp.exp(upd_log - m_new[..., None])
        decay = jnp.exp(b_last + m_prev - m_new)
        c_new = decay[..., None, None] * c_prev + jnp.einsum('bhs,bhsd,bhse->bhde', w_s, k_c, v_c)
        n_new = decay[..., None] * n_prev + jnp.einsum('bhs,bhsd->bhd', w_s, k_c)
        return (c_new, n_new, m_new), h_out

    init = (jnp.zeros((bsz, nh, d, d), jnp.float32),
            jnp.zeros((bsz, nh, d), jnp.float32),
            jnp.zeros((bsz, nh), jnp.float32))
    _, hs = lax.scan(step, init, (qc, kc, vc, ic, bc))
    return jnp.transpose(hs, (1, 0, 3, 2, 4)).reshape(bsz, seq, nh, d)


def setup_inputs(seed: int = 0) -> dict:
    key = jax.random.key(seed)
    ks = jax.random.split(key, 24)
    f32 = jnp.float32

    def nrm(k, shape, scale):
        return jax.random.normal(k, shape, f32) * scale

    def gain(k, shape):
        return 1.0 + 0.05 * jax.random.normal(k, shape, f32)

    return {
        'x': nrm(ks[0], (BATCH, SEQ, D_MODEL), 1.0),
        'p': nrm(ks[1], (DEPTH, BATCH, SEQ, PLE_DIM), 1.0),
        'g_mix': gain(ks[2], (DEPTH, D_MODEL)),
        'w_in': nrm(ks[3], (DEPTH, D_MODEL, IN_COLS), D_MODEL ** -0.5),
        'conv_w': nrm(ks[4], (DEPTH, CONV_K, 2 * ML_WIDTH), CONV_K ** -0.5),
        'conv_b': nrm(ks[5], (DEPTH, 2 * ML_WIDTH), 0.02),
        'b_i': nrm(ks[6], (DEPTH, ML_HEADS), 0.1) - 1.0,
        'b_f': nrm(ks[7], (DEPTH, ML_HEADS), 0.5) + 3.0,
        'lam_q1': nrm(ks[8], (DEPTH, DA_QK_DIM), 0.1),
        'lam_k1': nrm(ks[9], (DEPTH, DA_QK_DIM), 0.1),
        'lam_q2': nrm(ks[10], (DEPTH, DA_QK_DIM), 0.1),
        'lam_k2': nrm(ks[11], (DEPTH, DA_QK_DIM), 0.1),
        'da_sub_g': gain(ks[12], (DEPTH, DA_V_DIM)),
        'ml_norm_g': gain(ks[13], (DEPTH, ML_WIDTH)),
        'w_pa': nrm(ks[14], (DEPTH, DA_WIDTH, D_MODEL), DA_WIDTH ** -0.5),
        'w_pb': nrm(ks[15], (DEPTH, ML_WIDTH, D_MODEL), ML_WIDTH ** -0.5),
        'w_o': nrm(ks[16], (DEPTH, D_MODEL, D_MODEL), D_MODEL ** -0.5),
        'g_mlp': gain(ks[17], (DEPTH, D_MODEL)),
        'w_up': nrm(ks[18], (DEPTH, D_MODEL, D_FF), D_MODEL ** -0.5),
        'w_down': nrm(ks[19], (DEPTH, D_FF, D_MODEL), D_FF ** -0.5),
        'g_ple': gain(ks[20], (DEPTH, D_MODEL)),
        'w_ple_gate': nrm(ks[21], (DEPTH, D_MODEL, D_MODEL), D_MODEL ** -0.5),
        'w_ple_proj': nrm(ks[22], (DEPTH, PLE_DIM, D_MODEL), PLE_DIM ** -0.5),
        'g_final': gain(ks[23], (D_MODEL,)),
    }


def reference(x, p, g_mix, w_in, conv_w, conv_b, b_i, b_f, lam_q1, lam_k1, lam_q2, lam_k2,
              da_sub_g, ml_norm_g, w_pa, w_pb, w_o, g_mlp, w_up, w_down, g_ple, w_ple_gate,
              w_ple_proj, g_final):
    bsz, seq, _ = x.shape
    f32 = jnp.float32
    split_at = np.cumsum(SPLIT_SIZES)[:-1].tolist()
    ml_shape = (bsz, seq, ML_HEADS, ML_HEAD_DIM)
    for i in range(DEPTH):
        lambda_init = 0.8 - 0.6 * math.exp(-0.3 * i)
        h = rmsnorm(x, g_mix[i])
        (a_q, a_k, a_v, m_q, m_k, m_v, m_o, m_i, m_f, gate_a, gate_b) = jnp.split(
            h @ w_in[i], split_at, axis=-1)
        y_a = diff_attention(a_q, a_k, a_v, lam_q1[i], lam_k1[i], lam_q2[i], lam_k2[i],
                             da_sub_g[i], lambda_init)
        qk = causal_conv_silu(jnp.concatenate([m_q, m_k], axis=-1), conv_w[i], conv_b[i])
        m_q, m_k = jnp.split(qk, 2, axis=-1)
        log_i = (m_i + b_i[i]).astype(f32)
        log_f = jax.nn.log_sigmoid((m_f + b_f[i]).astype(f32))
        hm = mlstm_chunkwise(m_q.astype(f32).reshape(ml_shape), m_k.astype(f32).reshape(ml_shape),
                             m_v.astype(f32).reshape(ml_shape), log_i, log_f)
        hm = rmsnorm(hm, ml_norm_g[i].reshape(ML_HEADS, ML_HEAD_DIM)).reshape(bsz, seq, ML_WIDTH)
        y_b = (jax.nn.sigmoid(m_o.astype(f32)) * hm).astype(x.dtype)
        merged = (jax.nn.sigmoid(gate_a) * (y_a @ w_pa[i])
                  + jax.nn.sigmoid(gate_b) * (y_b @ w_pb[i]))
        x = x + merged @ w_o[i]
        u = rmsnorm(x, g_mlp[i]) @ w_up[i]
        x = x + jnp.square(jax.nn.relu(u)) @ w_down[i]
        ple_gate = jax.nn.sigmoid(rmsnorm(x, g_ple[i]) @ w_ple_gate[i])
        x = x + ple_gate * (p[i] @ w_ple_proj[i])
    return rmsnorm(x, g_final)
```

```python
import math
from contextlib import ExitStack
import numpy as np
import concourse.bass as bass
import concourse.mybir as mybir
from concourse.bass_utils import run_bass_kernel_spmd

F32 = mybir.dt.float32
BF16 = mybir.dt.bfloat16
U8 = mybir.dt.uint8
AF = mybir.ActivationFunctionType
ALU = mybir.AluOpType
AX = mybir.AxisListType
DTS = {F32: 4, BF16: 2, U8: 1}

S = 2048
D = 2048
T = 512
NT = S // T
KC = D // 128
EPS = 1e-6
LAMBDA_INIT = 0.8 - 0.6 * math.exp(0.0)
IN_COLS = 11272
C_AQ, C_AK, C_AV, C_MQ, C_MK, C_MV, C_MO, C_MI, C_MF, C_GA, C_GB = (
    0, 1024, 2048, 3072, 4096, 5120, 6144, 7168, 7172, 7176, 9224)
G = 128
USE_SCRATCH = False
ENGS = ("pe", "act", "dve", "pool", "sp")


class Tracker:
    def __init__(self):
        self.ops = []
        self.eng_ops = {e: [] for e in ENGS}
        self.count = {}
        self.gw = {}
        self.gr = {}
        self.seen = {e: {} for e in ENGS}
        self.sig = {e: set() for e in ENGS}

    @staticmethod
    def grans(ap):
        t = ap.tensor
        cn = t.__class__.__name__
        if cn.startswith("DRam"):
            if t.name.startswith("wscr"):
                return [("D", ap.offset // (128 * 4096))]
            return ()
        es = DTS[ap.dtype]
        dims = ap.ap
        pstep = dims[0][0]
        off = ap.offset % pstep if pstep else ap.offset
        span = 1 + sum((c - 1) * abs(s) for s, c in dims[1:])
        lo = off * es
        hi = (off + span) * es
        if not cn.startswith("SB"):
            return [(t.name, 0)]
        return [("S", g) for g in range(lo // G, (hi - 1) // G + 1)]

    def add(self, eng, fn, reads=(), writes=(), dma=None):
        key = eng if dma is None else "dma:" + dma
        idx = self.count.get(key, 0) + 1
        self.count[key] = idx
        deps = {}

        def dep(k, i, raw):
            if k == key and dma is None:
                if eng == "pe":
                    return
            if deps.get(k, 0) < i:
                deps[k] = i

        rg = set()
        for ap in reads:
            rg.update(self.grans(ap))
        wg = set()
        for ap in writes:
            wg.update(self.grans(ap))
        for g in rg:
            w = self.gw.get(g)
            if w:
                dep(w[0], w[1], True)
            if g[0] != "S":
                r = self.gr.get(g)
                if r:
                    for k, i in r.items():
                        if k != key:
                            dep(k, i, False)
        for g in wg:
            w = self.gw.get(g)
            if w:
                dep(w[0], w[1], False)
            r = self.gr.get(g)
            if r:
                for k, i in r.items():
                    dep(k, i, False)
        waits = []
        seen = self.seen[eng]
        for k, i in deps.items():
            if seen.get(k, 0) >= i:
                continue
            seen[k] = i
            waits.append((k, i))
            if not k.startswith("dma:"):
                self.sig[k].add(i)
        for g in rg:
            self.gr.setdefault(g, {})[key] = idx
        for g in wg:
            self.gw[g] = (key, idx)
            self.gr[g] = {}
        self.ops.append((eng, fn, waits, key, idx, dma))
        self.eng_ops[eng].append(len(self.ops) - 1)


class Builder:
    def __init__(self, debug=None):
        self.debug = debug
        self.nc = bass.Bass("TRN2", target_bir_lowering=False)
        self.tr = Tracker()
        self.es = ExitStack()
        self.dma_sems = {}
        self._rr = 0

    def op(self, eng, fn, reads=(), writes=(), dma=None):
        self.tr.add(eng, fn, reads, writes, dma)

    def act(self, out, in_, func, bias=None, scale=None, accum=None, extra_r=()):
        kw = {}
        rd = [in_] + list(extra_r)
        if bias is not None:
            kw["bias"] = bias
            if not isinstance(bias, float):
                rd.append(bias)
        if scale is not None:
            kw["scale"] = scale
            if not isinstance(scale, (float, int)):
                rd.append(scale)
        wr = [out]
        if accum is not None:
            kw["accum_out"] = accum
            wr.append(accum)
        self.op("act", lambda e: e.activation(out=out, in_=in_, func=func, **kw), rd, wr)

    def ts(self, eng, out, in0, s1, op0, s2=None, op1=None, accum=None):
        rd = [in0]
        if not isinstance(s1, (float, int)):
            rd.append(s1)
        if s2 is not None and not isinstance(s2, (float, int)):
            rd.append(s2)
        kw = {}
        if op1 is not None:
            kw["op1"] = op1
        wr = [out]
        if accum is not None:
            kw["accum_out"] = accum
            wr.append(accum)
        self.op(eng, lambda e: e.tensor_scalar(out=out, in0=in0, scalar1=s1, scalar2=s2, op0=op0, **kw), rd, wr)

    def tt(self, eng, out, in0, in1, op):
        self.op(eng, lambda e: e.tensor_tensor(out=out, in0=in0, in1=in1, op=op), [in0, in1], [out])

    def stt(self, eng, out, in0, scalar, in1, op0, op1, accum=None):
        rd = [in0, in1]
        if not isinstance(scalar, (float, int)):
            rd.append(scalar)
        kw = {}
        wr = [out]
        if accum is not None:
            kw["accum_out"] = accum
            wr.append(accum)
        self.op(eng, lambda e: e.scalar_tensor_tensor(out=out, in0=in0, scalar=scalar, in1=in1, op0=op0, op1=op1, **kw), rd, wr)

    def copy(self, eng, out, in_):
        if eng == "act":
            self.act(out, in_, AF.Copy)
        else:
            self.op(eng, lambda e: e.tensor_copy(out=out, in_=in_), [in_], [out])

    def evac(self, out, in_):
        self._rr ^= 1
        self.copy("act" if self._rr else "dve", out, in_)

    def memset(self, eng, ap, val):
        self.op(eng, lambda e: e.memset(ap, val), [], [ap])

    def mm(self, out, lhsT, rhs, start, stop, skip=False):
        kw = {"skip_group_check": True} if skip else {}
        self.op("pe", lambda e: e.matmul(out, lhsT=lhsT, rhs=rhs, start=start, stop=stop, **kw), [lhsT, rhs], [out])

    def tp(self, out, in_, ident):
        self.op("pe", lambda e: e.transpose(out=out, in_=in_, identity=ident), [in_, ident], [out])

    def dma(self, queue, sem, out, in_, nc_ok=False):
        if sem not in self.dma_sems:
            self.dma_sems[sem] = self.es.enter_context(self.nc.semaphore("d_" + sem))
        kw = {"allow_slow_non_contiguous": True} if nc_ok else {}
        self.op(queue, lambda e: e.dma_start(out=out, in_=in_, **kw), [in_], [out], dma=sem)

    def carve(self, off, shape, dt, p0=0):
        n = 1
        for s_ in shape[1:]:
            n *= s_
        nb = n * DTS[dt]
        assert off % 4 == 0 and off + nb <= self.big_bytes, (off, nb, self.big_bytes)
        ap = self.big[p0:p0 + shape[0], off:off + nb].bitcast(dt)
        if len(shape) == 3:
            ap = ap.rearrange("p (a b) -> p a b", a=shape[1], b=shape[2])
        elif len(shape) == 4:
            ap = ap.rearrange("p (a b c) -> p a b c", a=shape[1], b=shape[2], c=shape[3])
        return ap

    def emit(self, final_waits):
        nc = self.nc
        tr = self.tr
        sems = {e: self.es.enter_context(nc.semaphore("e_" + e)) for e in ENGS}
        rank = {}
        for e in ENGS:
            for r, i in enumerate(sorted(tr.sig[e])):
                rank[(e, i)] = r + 1
        block = self.es.enter_context(nc.Block())

        def run(eng_name):
            def body(e):
                for oi in tr.eng_ops[eng_name]:
                    _, fn, waits, key, idx, dma = tr.ops[oi]
                    for k, i in waits:
                        if k.startswith("dma:"):
                            e.wait_ge(self.dma_sems[k[4:]], 16 * i)
                        else:
                            e.wait_ge(sems[k], rank[(k, i)])
                    ins = fn(e)
                    if dma is not None:
                        ins.then_inc(self.dma_sems[dma], 16)
                    elif idx in tr.sig[eng_name]:
                        ins.then_inc(sems[eng_name], 1)
                if eng_name == "sp":
                    for sname in final_waits:
                        e.wait_ge(self.dma_sems[sname], 16 * tr.count["dma:" + sname])
            return body

        block.tensor(run("pe"))
        block.scalar(run("act"))
        block.vector(run("dve"))
        block.gpsimd(run("pool"))
        block.sync(run("sp"))


def build(debug=None):
    B = Builder(debug)
    nc = B.nc
    es = B.es
    dram = {}

    def din(name, shape):
        dram[name] = nc.dram_tensor(name, list(shape), F32, kind="ExternalInput").ap()
        return dram[name]

    x = din("x", (S, D))
    p = din("p", (S, 256))
    g_mix = din("g_mix", (1, D))
    w_in = din("w_in", (D, IN_COLS))
    conv_w = din("conv_w", (4, 2048))
    conv_b = din("conv_b", (1, 2048))
    b_i = din("b_i", (1, 4))
    b_f = din("b_f", (1, 4))
    lam_q1 = din("lam_q1", (1, 64))
    lam_k1 = din("lam_k1", (1, 64))
    lam_q2 = din("lam_q2", (1, 64))
    lam_k2 = din("lam_k2", (1, 64))
    da_sub_g = din("da_sub_g", (1, 128))
    ml_norm_g = din("ml_norm_g", (1, 1024))
    w_pa = din("w_pa", (1024, D))
    w_pb = din("w_pb", (1024, D))
    w_o = din("w_o", (D, D))
    g_mlp = din("g_mlp", (1, D))
    w_up = din("w_up", (D, 4 * D))
    w_down = din("w_down", (4 * D, D))
    g_ple = din("g_ple", (1, D))
    w_ple_gate = din("w_ple_gate", (D, D))
    w_ple_proj = din("w_ple_proj", (256, D))
    g_final = din("g_final", (1, D))
    out = nc.dram_tensor("out", [S, D], F32, kind="ExternalOutput").ap()
    dbg = None
    if debug:
        dbg = nc.dram_tensor("dbg", list(debug[1]), F32, kind="ExternalOutput").ap()

    al = lambda v: (v + G - 1) // G * G
    O_KT = 0
    O_V1 = al(O_KT + 32768)
    O_CST = al(O_V1 + 33024)
    O_CGB = al(O_CST + 8224)
    O_MLG = al(O_CGB + 4112)
    O_SM = al(O_MLG + 4096)
    O_W = al(O_SM + 8192)
    O_H = al(O_W + 32768)
    O_A = al(O_H + 16384)
    ARENA = 69632
    B.big_bytes = O_A + ARENA
    B.big = es.enter_context(nc.sbuf_tensor("big", [128, B.big_bytes], U8))
    cv = B.carve

    KT = cv(O_KT, [128, 8, S], BF16)
    V1 = cv(O_V1, [128, 16, 8, 129], BF16)
    Cst = cv(O_CST, [128, 4, 2, 257], F32)
    Cgb = cv(O_CGB, [128, 4, 2, 257], BF16)
    mlgc = cv(O_MLG, [128, 8], F32)
    hT = cv(O_H, [128, KC, T], BF16)
    wslots = [cv(O_W + 8192 * i, [128, 4096], BF16) for i in range(4)]

    sm = [O_SM]

    def small(shape, dt=F32, align=128):
        n = 1
        for s_ in shape[1:]:
            n *= s_
        nb = (n * DTS[dt] + align - 1) // align * align
        o = sm[0]
        sm[0] += nb
        assert sm[0] <= O_SM + 8192
        return cv(o, shape, dt)

    ident_b = small([128, 128], BF16)
    ident_f = small([128, 128], F32)
    mask01 = small([64, 64], F32)
    subg = small([128, 128], F32)
    lamt = small([128, 4, 64], F32)
    lams = small([128, 8], F32)
    convw = small([128, 16, 4], F32)
    convb = small([128, 16], F32)
    gmix = small([128, 16], F32)
    gmlp = small([128, 16], F32)
    gple = small([128, 16], F32)
    bif = small([4, 4], F32)
    ones4 = small([4, 128], F32)
    cst = small([128, 4], F32)
    halo = small([128, 16, 3], F32)
    alpha_tok = small([64, 8, 4], F32)
    eb_tok = small([64, 8, 4], F32)
    gam_bc = small([128, 4, 8], F32)
    mp = small([4, 16], F32)
    ssn = small([128, 8], F32)
    ssa = small([128, 2, 32], F32)
    ssm = small([64, 3, 8], F32)
    sc1 = small([128, 8], F32)
    gsm = small([4, 6, 8], F32)

    psb = [es.enter_context(nc.psum_tensor(f"pb{i}", [128, 512], F32)) for i in range(8)]

    def bank(i):
        return psb[i][:, :]

    def bank_bf(i):
        return psb[i][:, :].bitcast(BF16)

    NBLK = 145
    wscr = nc.dram_tensor("wscr", [NBLK * 128, 4096], BF16, kind="Internal").ap() if USE_SCRATCH else None
    wstate = {"sched": [], "issued": 0, "next": 0}

    def wsched_add(tag, src, kcn, ncols):
        wstate["sched"].append((tag, src, kcn, ncols))

    def wview(k):
        tag, src, kcn, ncols = wstate["sched"][k]
        return wslots[k % 4][:, 0:kcn * ncols].rearrange("p (a b) -> p a b", a=kcn, b=ncols)

    def wnext(tag):
        k = wstate["next"]
        assert wstate["sched"][k][0] == tag, (wstate["sched"][k][0], tag)
        nblk = len(wstate["sched"]) // NT
        while wstate["issued"] < min(len(wstate["sched"]), k + 4):
            j = wstate["issued"]
            _, src, kcn, ncols = wstate["sched"][j]
            n_ = kcn * ncols
            jb = j % nblk
            flat = wslots[j % 4][:, 0:n_]
            if j < nblk or not USE_SCRATCH:
                B.dma("pool", f"w{j % 4}", wview(j), src.rearrange("(kc p) n -> p kc n", p=128))
                if USE_SCRATCH:
                    B.dma("sp", f"wb{j % 4}", wscr[jb * 128:(jb + 1) * 128, 0:n_], flat)
            else:
                B.dma("sp", f"w{j % 4}", flat, wscr[jb * 128:(jb + 1) * 128, 0:n_])
            wstate["issued"] += 1
        wstate["next"] += 1
        return wview(k)

    for t in range(NT):
        wsched_add("gif", w_in[:, C_MI:C_MI + 8], 16, 8)
        for b_ in range(4):
            wsched_add("ak", w_in[:, C_AK + 256 * b_:C_AK + 256 * (b_ + 1)], 16, 256)
        for cb in range(2):
            for kh in range(2):
                wsched_add("av", w_in[kh * 1024:(kh + 1) * 1024, C_AV + 512 * cb:C_AV + 512 * (cb + 1)], 8, 512)
        for b_ in range(4):
            wsched_add("aq", w_in[:, C_AQ + 256 * b_:C_AQ + 256 * (b_ + 1)], 16, 256)
        for h in range(4):
            for c0 in (C_MQ, C_MK):
                wsched_add("ml", w_in[:, c0 + 256 * h:c0 + 256 * (h + 1)], 16, 256)
        for h in range(4):
            for c0 in (C_MV, C_MO):
                wsched_add("ml", w_in[:, c0 + 256 * h:c0 + 256 * (h + 1)], 16, 256)
        for ng in range(8):
            wsched_add("ga", w_in[:, C_GA + 256 * ng:C_GA + 256 * (ng + 1)], 16, 256)
            wsched_add("gb", w_in[:, C_GB + 256 * ng:C_GB + 256 * (ng + 1)], 16, 256)
            wsched_add("pa", w_pa[:, 256 * ng:256 * (ng + 1)], 8, 256)
            wsched_add("pb", w_pb[:, 256 * ng:256 * (ng + 1)], 8, 256)
        for cb in range(4):
            for kh in range(2):
                wsched_add("wo", w_o[kh * 1024:(kh + 1) * 1024, 512 * cb:512 * (cb + 1)], 8, 512)
        for hf in range(2):
            for b_ in range(16):
                c0 = hf * 4096 + 256 * b_
                wsched_add("up", w_up[:, c0:c0 + 256], 16, 256)
            for cb in range(4):
                for kb in range(4):
                    r0 = hf * 4096 + kb * 1024
                    wsched_add("dn", w_down[r0:r0 + 1024, 512 * cb:512 * (cb + 1)], 8, 512)
        for cb in range(4):
            wsched_add("pp", w_ple_proj[:, 512 * cb:512 * (cb + 1)], 2, 512)
            for kh in range(2):
                wsched_add("pg", w_ple_gate[kh * 1024:(kh + 1) * 1024, 512 * cb:512 * (cb + 1)], 8, 512)

    B.memset("pool", ident_b, 0.0)
    B.op("pool", lambda e: e.affine_select(out=ident_b, in_=ident_b, pattern=[[-1, 128]], compare_op=ALU.not_equal,
                                           fill=1.0, base=0, channel_multiplier=1), [ident_b], [ident_b])
    B.memset("pool", ident_f, 0.0)
    B.op("pool", lambda e: e.affine_select(out=ident_f, in_=ident_f, pattern=[[-1, 128]], compare_op=ALU.not_equal,
                                           fill=1.0, base=0, channel_multiplier=1), [ident_f], [ident_f])
    B.memset("pool", mask01, 1.0)
    B.op("pool", lambda e: e.affine_select(out=mask01, in_=mask01, pattern=[[1, 64]], compare_op=ALU.is_ge,
                                           fill=0.0, base=0, channel_multiplier=-1), [mask01], [mask01])
    B.memset("pool", ones4, 1.0)
    B.memset("pool", cst[:, 0:1], EPS)
    B.memset("pool", cst[:, 1:2], 1.0)
    B.memset("pool", halo, 0.0)
    B.memset("pool", mp, 0.0)
    B.memset("pool", Cst, 0.0)
    B.memset("pool", V1[:, :, :, 128:129], 1.0)

    B.dma("sp", "c0", subg, da_sub_g[0:1, :].partition_broadcast(128))
    for i_, l_ in enumerate((lam_q1, lam_k1, lam_q2, lam_k2)):
        B.dma("sp", f"c{1 + i_}", lamt[:, i_, :], l_[0:1, :].partition_broadcast(128))
    for j_ in range(4):
        B.dma("sp", f"c5{j_}", convw[:, :, j_], conv_w[j_, :].rearrange("(kc p) -> p kc", p=128), nc_ok=True)
    B.dma("sp", "c6", convb, conv_b[0, :].rearrange("(kc p) -> p kc", p=128), nc_ok=True)
    B.dma("sp", "c7", gmix, g_mix[0, :].rearrange("(kc p) -> p kc", p=128), nc_ok=True)
    B.dma("sp", "c8", gmlp, g_mlp[0, :].rearrange("(kc p) -> p kc", p=128), nc_ok=True)
    B.dma("sp", "c9", gple, g_ple[0, :].rearrange("(kc p) -> p kc", p=128), nc_ok=True)
    B.dma("sp", "c10", bif[:, 0:1], b_i.rearrange("o h -> h o"), nc_ok=True)
    B.dma("sp", "c11", bif[:, 1:2], b_f.rearrange("o h -> h o"), nc_ok=True)
    B.dma("sp", "c12", mlgc, ml_norm_g[0, :].rearrange("(kc p) -> p kc", p=128), nc_ok=True)
    B.ts("dve", subg, subg, 1.0 - LAMBDA_INIT, ALU.mult)
    B.ts("dve", mlgc, mlgc, 0.5, ALU.mult)
    B.ts("dve", bif[:, 2:3], bif[:, 1:2], -1.0, ALU.mult)
    B.ts("dve", convw[:, 0:8, :], convw[:, 0:8, :], 1.0 / 32.0, ALU.mult)
    B.ts("dve", convw[:, 8:16, :], convw[:, 8:16, :], 0.5, ALU.mult)
    B.ts("dve", convb[:, 0:8], convb[:, 0:8], 1.0 / 32.0, ALU.mult)
    B.ts("dve", convb[:, 8:16], convb[:, 8:16], 0.5, ALU.mult)
    B.memset("dve", lams, 0.0)
    B.tt("dve", lamt[:, 0, :], lamt[:, 0, :], lamt[:, 1, :], ALU.mult)
    B.tt("dve", lamt[:, 2, :], lamt[:, 2, :], lamt[:, 3, :], ALU.mult)
    B.op("dve", lambda e: e.reduce_sum(out=lams[:, 0:1], in_=lamt[:, 0, :], axis=AX.X), [lamt[:, 0, :]], [lams[:, 0:1]])
    B.op("dve", lambda e: e.reduce_sum(out=lams[:, 1:2], in_=lamt[:, 2, :], axis=AX.X), [lamt[:, 2, :]], [lams[:, 1:2]])
    B.act(lams[:, 2:4], lams[:, 0:2], AF.Exp)
    B.tt("dve", lams[:, 4:5], lams[:, 3:4], lams[:, 2:3], ALU.subtract)
    B.ts("dve", lams[:, 4:5], lams[:, 4:5], -LAMBDA_INIT, ALU.add)
    neglam = lams[:, 4:5]

    def rstd(dst, src, inv_n, tmp):
        B.act(tmp, src, AF.Ln, bias=cst[0:src.shape[0], 0:1], scale=inv_n)
        B.act(dst, tmp, AF.Exp, scale=-0.5)

    def norm_to_hT(srcs, gcol, xs):
        B.memset("dve", ssn[:, 0:4], 0.0)
        for i, s_ in enumerate(srcs):
            B.act(xs[:, i, :], s_, AF.Square, accum=ssn[:, i:i + 1])
        rstd(ssn[:, 4:8], ssn[:, 0:4], 1.0 / D, sc1[:, 0:4])
        for i, s_ in enumerate(srcs):
            if i % 2 == 0:
                B.act(xs[:, i, :], s_, AF.Copy, scale=ssn[:, 4 + i:5 + i])
            else:
                B.ts("dve", xs[:, i, :], s_, ssn[:, 4 + i:5 + i], ALU.mult)
        for kc in range(KC):
            pb_ = 6 + (kc % 2)
            for i in range(4):
                B.tp(bank_bf(pb_)[:, i * 128:(i + 1) * 128], xs[:, i, kc * 128:(kc + 1) * 128], ident_b)
            if kc % 2 == 0:
                B.ts("dve", hT[:, kc, :], bank_bf(pb_)[:, 0:512], gcol[:, kc:kc + 1], ALU.mult)
            else:
                B.act(hT[:, kc, :], bank_bf(pb_)[:, 0:512], AF.Copy, scale=gcol[:, kc:kc + 1])

    dense_rr = [0]

    def dbank(nb=6):
        dense_rr[0] = (dense_rr[0] + 1) % nb
        return dense_rr[0]

    def ws_chunk(wv, n, rhs_of_kc, nk, N=T):
        pb_ = dbank()
        o = bank(pb_)[:, 0:N]
        for kc in range(nk):
            B.mm(o, wv[:, kc, n * 128:(n + 1) * 128], rhs_of_kc(kc), kc == 0, kc == nk - 1)
        return o

    def dump(ap, r0, c0):
        P_, F_ = ap.shape[0], ap.shape[1]
        dump.n = getattr(dump, "n", 0) + 1
        B.dma("sp", f"dbg{dump.n}", dbg[r0:r0 + P_, c0:c0 + F_], ap)

    for t in range(NT):
        tok0 = t * T
        A = O_A
        xs = cv(A, [128, 4, D], BF16)
        xld = [cv(A + 16384 + 8192 * i, [128, D], F32) for i in range(2)]
        B.memset("dve", ssn[:, 0:4], 0.0)
        for i in range(4):
            xb_ = xld[i % 2]
            B.dma("sp", f"x{i % 2}", xb_, x[tok0 + i * 128:tok0 + (i + 1) * 128, :])
            B.act(xs[:, i, :], xb_, AF.Square, accum=ssn[:, i:i + 1])
        rstd(ssn[:, 4:8], ssn[:, 0:4], 1.0 / D, sc1[:, 0:4])
        for i in range(4):
            xb_ = xld[i % 2]
            B.dma("sp", f"x{i % 2}", xb_, x[tok0 + i * 128:tok0 + (i + 1) * 128, :])
            if i % 2 == 0:
                B.act(xs[:, i, :], xb_, AF.Copy, scale=ssn[:, 4 + i:5 + i])
            else:
                B.ts("dve", xs[:, i, :], xb_, ssn[:, 4 + i:5 + i], ALU.mult)
        for kc in range(KC):
            pb_ = 6 + (kc % 2)
            for i in range(4):
                B.tp(bank_bf(pb_)[:, i * 128:(i + 1) * 128], xs[:, i, kc * 128:(kc + 1) * 128], ident_b)
            if kc % 2 == 0:
                B.ts("dve", hT[:, kc, :], bank_bf(pb_)[:, 0:512], gmix[:, kc:kc + 1], ALU.mult)
            else:
                B.act(hT[:, kc, :], bank_bf(pb_)[:, 0:512], AF.Copy, scale=gmix[:, kc:kc + 1])
        if debug and debug[0] == "hT" and t == 0:
            tmpf = cv(A + 32768, [128, T], F32)
            for kc in range(KC):
                B.copy("dve", tmpf, hT[:, kc, :])
                dump(tmpf, kc * 128, 0)
            break

        GA_ = A + 32768
        wv = wnext("gif")
        for gi in range(2):
            o = bank(gi)[0:4, 0:T]
            for kc in range(KC):
                B.mm(o, wv[:, kc, 4 * gi:4 * gi + 4], hT[:, kc, :], kc == 0, kc == KC - 1)
        li = cv(GA_, [4, 8, 64], F32)
        cs0 = cv(GA_ + 2048, [4, 8, 64], F32)
        cs1 = cv(GA_ + 4096, [4, 8, 64], F32)
        av_ = cv(GA_ + 6144, [4, 8, 64], F32)
        alph = cv(GA_ + 8192, [4, 8, 64], F32)
        ebf = cv(GA_ + 10240, [4, 8, 64], F32)
        Gm = cv(GA_ + 12288, [4, 4, 8], F32)
        pgi = bank(0)[0:4, 0:T].rearrange("p (c l) -> p c l", c=8, l=64)
        pgf = bank(1)[0:4, 0:T].rearrange("p (c l) -> p c l", c=8, l=64)
        B.ts("dve", li, pgi, bif[:, 0:1], ALU.add)
        B.act(cs1, pgf, AF.Exp, bias=bif[:, 2:3], scale=-1.0)
        B.act(cs0, cs1, AF.Ln, bias=cst[0:4, 1:2], scale=1.0)
        X_, Y_ = cs0, cs1
        for d_ in (1, 2, 4, 8, 16, 32):
            B.copy("dve", Y_[:, :, 0:d_], X_[:, :, 0:d_])
            B.tt("dve", Y_[:, :, d_:64], X_[:, :, d_:64], X_[:, :, 0:64 - d_], ALU.add)
            X_, Y_ = Y_, X_
        csum = X_
        B.tt("dve", av_, li, csum, ALU.add)
        A_c, Bl, Mc, nMc, garg, gam = (gsm[:, k_, :] for k_ in range(6))
        B.op("dve", lambda e: e.tensor_reduce(out=A_c, in_=av_, axis=AX.X, op=ALU.max), [av_], [A_c])
        B.ts("dve", Bl, csum[:, :, 63], -1.0, ALU.mult)
        for c in range(8):
            B.tt("dve", Mc[:, c:c + 1], mp[:, c:c + 1], A_c[:, c:c + 1], ALU.max)
            B.tt("dve", mp[:, c + 1:c + 2], Bl[:, c:c + 1], Mc[:, c:c + 1], ALU.add)
        B.tt("dve", garg, mp[:, 0:8], Mc, ALU.subtract)
        B.ts("dve", nMc, Mc, -1.0, ALU.mult)
        B.act(gam, garg, AF.Exp)
        for c in range(8):
            B.act(alph[:, c, :], av_[:, c, :], AF.Exp, bias=nMc[:, c:c + 1], scale=1.0)
            B.act(ebf[:, c, :], csum[:, c, :], AF.Exp, bias=nMc[:, c:c + 1], scale=1.0)
        B.copy("dve", mp[:, 0:1], mp[:, 8:9])
        for c in range(8):
            B.tp(bank(2)[0:64, c * 4:(c + 1) * 4], alph[:, c, :], ident_f[0:4, 0:4])
            B.tp(bank(3)[0:64, c * 4:(c + 1) * 4], ebf[:, c, :], ident_f[0:4, 0:4])
        B.copy("dve", alpha_tok, bank(2)[0:64, 0:32].rearrange("p (c h) -> p c h", c=8, h=4))
        B.copy("dve", eb_tok, bank(3)[0:64, 0:32].rearrange("p (c h) -> p c h", c=8, h=4))
        for h in range(4):
            B.ts("dve", Gm[:, h, :], gam, ident_f[0:4, h:h + 1], ALU.mult)
        B.mm(bank(4)[:, 0:32], ones4, Gm.rearrange("p a b -> p (a b)"), True, True)
        B.copy("dve", gam_bc, bank(4)[:, 0:32].rearrange("p (a b) -> p a b", a=4, b=8))

        if debug and debug[0] == "gates" and t == debug[2]:
            dump(gam_bc.rearrange("p a b -> p (a b)"), 0, 0)
            dump(alpha_tok.rearrange("p a b -> p (a b)"), 128, 0)
            dump(eb_tok.rearrange("p a b -> p (a b)"), 192, 0)
            dump(mp, 256, 0)
            dump(gsm.rearrange("p a b -> p (a b)"), 260, 0)
            break
        QT = cv(A + 45312, [128, 8, T], BF16)
        for b_ in range(4):
            wv = wnext("ak")
            for n in range(2):
                o = ws_chunk(wv, n, lambda kc: hT[:, kc, :], KC)
                B.evac(KT[:, 2 * b_ + n, tok0:tok0 + T], o)
        for cb in range(2):
            accs = [bank(i) for i in range(4)]
            for kh in range(2):
                wv = wnext("av")
                for i in range(4):
                    for kc in range(8):
                        B.mm(accs[i], hT[:, kh * 8 + kc, i * 128:(i + 1) * 128], wv[:, kc, :],
                             kh == 0 and kc == 0, kh == 1 and kc == 7)
            for i in range(4):
                B.evac(V1[:, 4 * t + i, 4 * cb:4 * cb + 4, 0:128], accs[i].rearrange("p (h d) -> p h d", h=4, d=128))
        for b_ in range(4):
            wv = wnext("aq")
            for n in range(2):
                o = ws_chunk(wv, n, lambda kc: hT[:, kc, :], KC)
                B.evac(QT[:, 2 * b_ + n, :], o)

        ya_tok = cv(A + 45312, [128, 4, 1024], BF16)
        yaT = cv(A + 16384, [128, 8, T], BF16)
        ybT = cv(A + 24576, [128, 8, T], BF16)
        ET = [cv(A + 53504 + 2048 * i, [128, 2, T], BF16) for i in range(2)]
        o_all = cv(A, [128, 4, 8, 128], F32)
        dtmp = cv(A + 57600, [128, 128], F32)
        sqj = cv(A + 58112, [128, 128], BF16)
        nkb = 4 * t + 4
        B.memset("dve", ssa[:, 0, :], 0.0)
        def fin_block(i, h):
            acc = bank(4 + i)[:, 0:258].rearrange("p (m d) -> p m d", m=2, d=129)
            rs = sc1[:, 0:2]
            B.op("dve", lambda e, acc=acc, rs=rs: e.reciprocal(out=rs, in_=acc[:, :, 128]), [acc[:, :, 128]], [rs])
            B.tt("dve", sc1[:, 2:3], sc1[:, 1:2], neglam, ALU.mult)
            B.ts("dve", dtmp, acc[:, 0, 0:128], sc1[:, 0:1], ALU.mult)
            B.stt("dve", o_all[:, i, h, :], acc[:, 1, 0:128], sc1[:, 2:3], dtmp, ALU.mult, ALU.add)
            B.act(sqj, o_all[:, i, h, :], AF.Square, accum=ssa[:, 0, i * 8 + h:i * 8 + h + 1])

        jj = 0
        for h in range(8):
            def s_mm(j, jj_):
                q0_ = max(0, j - 4 * t) * 128
                for m in range(2):
                    B.mm(bank((jj_ % 2) * 2 + m)[:, q0_:T], KT[64 * m:64 * m + 64, h, j * 128:(j + 1) * 128],
                         QT[64 * m:64 * m + 64, h, q0_:T], True, True)
            s_mm(0, jj)
            for j in range(nkb):
                qi0 = max(0, j - 4 * t)
                q0 = qi0 * 128
                eb_ = ET[jj % 2]
                sb0 = (jj % 2) * 2
                for m in range(2):
                    B.act(eb_[:, m, q0:T], bank(sb0 + m)[:, q0:T], AF.Exp, scale=0.125)
                if j >= 4 * t:
                    B.memset("dve", eb_[64:128, :, q0:q0 + 64], 0.0)
                if j - 1 >= 4 * t:
                    fin_block(j - 1 - 4 * t, h)
                if j + 1 < nkb:
                    s_mm(j + 1, jj + 1)
                for i in range(qi0, 4):
                    for m in range(2):
                        B.mm(bank(4 + i)[:, m * 129:(m + 1) * 129], eb_[:, m, i * 128:(i + 1) * 128],
                             V1[:, j, h, :], j == 0 and m == 0, j == 4 * t + i and m == 1, skip=True)
                jj += 1
            fin_block(3, h)
        rstd(ssa[:, 1, :], ssa[:, 0, :], 1.0 / 128.0, ssa[:, 0, :])
        for i in range(4):
            for h in range(8):
                B.stt("dve", ya_tok[:, i, h * 128:(h + 1) * 128], o_all[:, i, h, :], ssa[:, 1, i * 8 + h:i * 8 + h + 1],
                      subg, ALU.mult, ALU.mult)
        for kc in range(8):
            pb_ = 6 + (kc % 2)
            for i in range(4):
                B.tp(bank_bf(pb_)[:, i * 128:(i + 1) * 128], ya_tok[:, i, kc * 128:(kc + 1) * 128], ident_b)
            B.evac(yaT[:, kc, :], bank_bf(pb_)[:, 0:512])
        if debug and debug[0] == "yaT" and t == debug[2]:
            tmpf = cv(A + 32768, [128, T], F32)
            for kc in range(8):
                B.copy("dve", tmpf, yaT[:, kc, :])
                dump(tmpf, kc * 128, 0)
            break

        M0 = A + 32768
        raw = [cv(M0 + 2176 * i, [128, 515], F32) for i in range(2)]
        accb = [cv(M0 + 4352 + 2048 * i, [128, T], F32) for i in range(2)]
        thb = [cv(M0 + 8448 + 2048 * i, [128, T], F32) for i in range(2)]
        mkT_all = cv(M0 + 12544, [128, 8, T], BF16)
        stg = cv(M0 + 20736, [128, 2, T], BF16)
        mk_tok = cv(M0 + 22784, [64, 8, 256], BF16)
        vpr = cv(M0 + 26880, [64, 8, 257], BF16)
        tho = cv(M0 + 31104, [64, 8, 256], BF16)
        yb = cv(A, [64, 8, 256], BF16)
        Pm = cv(A + 4096, [64, 8, 64], BF16)
        g2 = cv(A + 5120, [64, 256], F32)
        hm2 = [cv(A + 6144 + 1024 * i, [64, 256], F32) for i in range(2)]
        mqT_all = cv(A + 8192, [128, 8, T], BF16)
        Cg2 = cv(O_CGB, [128, 2, 2, 257], BF16)
        def tok_transposes(src, n):
            pv = bank_bf(6 + n)[0:64, :].rearrange("p (c d) -> p c d", c=8, d=128)
            for c in range(8):
                B.tp(pv[:, c, :], src[:, n, c * 64:(c + 1) * 64], ident_b)
            return pv

        def prologue_qk(h):
            for kind in range(2):
                wv = wnext("ml")
                for n in range(2):
                    o = ws_chunk(wv, n, lambda kc: hT[:, kc, :], KC)
                    kcg = kind * 8 + h * 2 + n
                    rw = raw[n]
                    ac = accb[n]
                    B.copy("dve", rw[:, 0:3], halo[:, kcg, :])
                    B.copy("act", rw[:, 3:515], o)
                    B.copy("dve", halo[:, kcg, :], rw[:, 512:515])
                    B.act(ac, rw[:, 3:515], AF.Identity, bias=convb[:, kcg:kcg + 1], scale=convw[:, kcg, 3:4])
                    for j_ in (2, 1, 0):
                        B.stt("dve", ac, rw[:, j_:j_ + T], convw[:, kcg, j_:j_ + 1], ac, ALU.mult, ALU.add)
                    B.act(thb[n], ac, AF.Tanh, scale=16.0 if kind == 0 else 1.0)
                    dst = mqT_all if kind == 0 else mkT_all
                    B.stt("dve", dst[:, 2 * h + n, :], thb[n], 1.0, ac, ALU.add, ALU.mult)

        def mk_to_tok(h):
            for n in range(2):
                pv = tok_transposes(mkT_all[:, 2 * h:2 * h + 2, :], n)
                B.evac(mk_tok[:, :, n * 128:(n + 1) * 128], pv)

        def prologue_vo(h):
            for kind in (2, 3):
                wv = wnext("ml")
                for n in range(2):
                    o = ws_chunk(wv, n, lambda kc: hT[:, kc, :], KC)
                    if kind == 2:
                        B.evac(stg[:, n, :], o)
                    else:
                        B.act(stg[:, n, :], o, AF.Tanh, scale=0.5)
                for n in range(2):
                    pv = tok_transposes(stg, n)
                    if kind == 2:
                        for c in range(8):
                            B.ts("dve", vpr[:, c, n * 128:(n + 1) * 128], pv[:, c, :], alpha_tok[:, c, h:h + 1], ALU.mult)
                    else:
                        B.evac(tho[:, :, n * 128:(n + 1) * 128], pv)
                if kind == 2:
                    B.copy("dve", vpr[:, :, 256], alpha_tok[:, :, h])

        def chunk_loop(h):
            B.memset("dve", ssm[:, 0, :], 0.0)
            psPa = bank(0)[0:64, :].rearrange("p (c l) -> p c l", c=8, l=64)
            for c in range(8):
                cs = slice(c * 64, (c + 1) * 64)
                for n in range(2):
                    B.mm(psPa[:, c, :], mkT_all[:, 2 * h + n, cs], mqT_all[:, 2 * h + n, cs], n == 0, n == 1)
            for c in range(8):
                B.tt("dve", Pm[:, c, :], psPa[:, c, :], mask01, ALU.mult)

            def finalize(c):
                psN = bank(1 + (c % 2))[0:64, 0:257]
                dm = sc1[0:64, 4:5]
                B.act(sc1[0:64, 6:7], psN[:, 256:257], AF.Abs)
                B.ts("dve", dm, sc1[0:64, 6:7], eb_tok[:, c, h:h + 1], ALU.max)
                B.op("dve", lambda e, dm=dm: e.reciprocal(out=sc1[0:64, 5:6], in_=dm), [dm], [sc1[0:64, 5:6]])
                hmb = hm2[c % 2]
                B.act(hmb, psN[:, 0:256], AF.Copy, scale=sc1[0:64, 5:6])
                B.act(g2, hmb, AF.Square, accum=ssm[:, 0, c:c + 1])
                rstd(ssm[:, 1, c:c + 1], ssm[:, 0, c:c + 1], 1.0 / 256.0, ssm[:, 2, c:c + 1])
                B.act(hmb, hmb, AF.Copy, scale=ssm[:, 1, c:c + 1])
                B.stt("dve", yb[:, c, :], tho[:, c, :], 1.0, hmb, ALU.add, ALU.mult)

            for c in range(8):
                cs = slice(c * 64, (c + 1) * 64)
                par = c % 2
                gcol = gam_bc[:, h, c:c + 1]
                psCs = [bank(3 + 2 * par + n)[:, 0:257] for n in range(2)]
                for n in range(2):
                    B.mm(psCs[n], mk_tok[:, c, n * 128:(n + 1) * 128], vpr[:, c, :], True, True)
                B.ts("dve", Cg2[:, par, :, :], Cst[:, h, :, :], gcol, ALU.mult)
                for n in range(2):
                    B.stt("dve", Cst[:, h, n, :], Cst[:, h, n, :], gcol, psCs[n], ALU.mult, ALU.add)
                psN = bank(1 + par)[0:64, 0:257]
                B.mm(psN, Pm[:, c, :], vpr[:, c, :], True, False)
                for n in range(2):
                    B.mm(psN, mqT_all[:, 2 * h + n, cs], Cg2[:, par, n, :], False, n == 1)
                if c >= 1:
                    finalize(c - 1)
            finalize(7)

        def epilogue(h):
            for n in range(2):
                pb_ = 6 + n
                for c in range(8):
                    B.tp(bank_bf(pb_)[:, c * 64:(c + 1) * 64], yb[:, c, n * 128:(n + 1) * 128], ident_b[0:64, 0:64])
                B.act(ybT[:, 2 * h + n, :], bank_bf(pb_)[:, 0:512], AF.Copy, scale=mlgc[:, 2 * h + n:2 * h + n + 1])

        for h in range(4):
            prologue_qk(h)
        for h in range(4):
            prologue_vo(h)
            mk_to_tok(h)
            chunk_loop(h)
            epilogue(h)
        if debug and debug[0] == "ybT" and t == debug[2]:
            tmpf = cv(A + 32768, [128, T], F32)
            for kc in range(8):
                B.copy("dve", tmpf, ybT[:, kc, :])
                dump(tmpf, kc * 128, 0)
            break

        mergedT = cv(A, [128, KC, T], BF16)
        tmpA = [cv(A + 32768 + 2048 * i, [128, T], F32) for i in range(2)]
        tmpB = [cv(A + 36864 + 2048 * i, [128, T], F32) for i in range(2)]
        for ng in range(8):
            wv = wnext("ga")
            for n in range(2):
                o = ws_chunk(wv, n, lambda kc: hT[:, kc, :], KC)
                B.act(tmpA[n], o, AF.Tanh, scale=0.5)
            wv = wnext("gb")
            for n in range(2):
                o = ws_chunk(wv, n, lambda kc: hT[:, kc, :], KC)
                B.act(tmpB[n], o, AF.Tanh, scale=0.5)
            wv = wnext("pa")
            for n in range(2):
                o = ws_chunk(wv, n, lambda kc: yaT[:, kc, :], 8)
                B.stt("dve", tmpA[n], tmpA[n], 1.0, o, ALU.add, ALU.mult)
            wv = wnext("pb")
            for n in range(2):
                o = ws_chunk(wv, n, lambda kc: ybT[:, kc, :], 8)
                B.stt("dve", tmpB[n], tmpB[n], 1.0, o, ALU.add, ALU.mult)
                B.tt("dve", mergedT[:, 2 * ng + n, :], tmpA[n], tmpB[n], ALU.add)

        xres = cv(A + 32768, [128, 4, D], F32)
        xq = [cv(A + 16384 + 2048 * i, [128, 512], F32) for i in range(4)]
        xqi = 0
        for cb in range(4):
            base = 4 * (cb % 2)
            for kh in range(2):
                wv = wnext("wo")
                for i in range(4):
                    for kc in range(8):
                        B.mm(bank(base + i), mergedT[:, kh * 8 + kc, i * 128:(i + 1) * 128], wv[:, kc, :],
                             kh == 0 and kc == 0, kh == 1 and kc == 7)
            for i in range(4):
                xb_ = xq[xqi % 4]
                B.dma("sp", f"xq{xqi % 4}", xb_, x[tok0 + i * 128:tok0 + (i + 1) * 128, cb * 512:(cb + 1) * 512])
                xqi += 1
                B.stt("dve", xres[:, i, cb * 512:(cb + 1) * 512], bank(base + i), 0.5, xb_, ALU.mult, ALU.add)
        if debug and debug[0] == "x1" and t == debug[2]:
            for i in range(4):
                dump(xres[:, i, :], i * 128, 0)
            break

        xs = cv(A, [128, 4, D], BF16)
        norm_to_hT([xres[:, i, :] for i in range(4)], gmlp, xs)

        uT = cv(A, [128, 32, T], BF16)
        rt = [cv(A + 65536 + 2048 * i, [128, T], F32) for i in range(2)]
        for hf in range(2):
            for b_ in range(16):
                wv = wnext("up")
                for n in range(2):
                    o = ws_chunk(wv, n, lambda kc: hT[:, kc, :], KC)
                    r_ = rt[n]
                    B.act(r_, o, AF.Relu)
                    B.tt("dve", uT[:, 2 * b_ + n, :], r_, r_, ALU.mult)
            for cb in range(4):
                base = 4 * (cb % 2)
                for kb in range(4):
                    wv = wnext("dn")
                    for i in range(4):
                        for kc in range(8):
                            B.mm(bank(base + i), uT[:, kb * 8 + kc, i * 128:(i + 1) * 128], wv[:, kc, :],
                                 kb == 0 and kc == 0, kb == 3 and kc == 7)
                for i in range(4):
                    xr = xres[:, i, cb * 512:(cb + 1) * 512]
                    B.tt("dve", xr, xr, bank(base + i), ALU.add)
        if debug and debug[0] == "x2" and t == debug[2]:
            for i in range(4):
                dump(xres[:, i, :], i * 128, 0)
            break

        xs = cv(A, [128, 4, D], BF16)
        norm_to_hT([xres[:, i, :] for i in range(4)], gple, xs)
        pld = cv(A + 16384, [128, 4, 256], F32)
        pbf = cv(A + 20480, [128, 4, 256], BF16)
        pT = cv(A + 22528, [128, 2, T], BF16)
        ppt = [cv(A + 2048 * i, [128, 512], F32) for i in range(4)]
        tgt = [cv(A + 24576 + 2048 * i, [128, 512], F32) for i in range(4)]
        B.dma("sp", "pl", pld, p[tok0:tok0 + T, :].rearrange("(i p) c -> p i c", p=128))
        B.copy("dve", pbf, pld)
        for kc in range(2):
            pb_ = 6 + kc
            for i in range(4):
                B.tp(bank_bf(pb_)[:, i * 128:(i + 1) * 128], pbf[:, i, kc * 128:(kc + 1) * 128], ident_b)
            B.evac(pT[:, kc, :], bank_bf(pb_)[:, 0:512])
        for cb in range(4):
            base = 4 * (cb % 2)
            wv = wnext("pp")
            for i in range(4):
                for kc in range(2):
                    B.mm(bank(base + i), pT[:, kc, i * 128:(i + 1) * 128], wv[:, kc, :], kc == 0, kc == 1)
            for i in range(4):
                B.copy("act", ppt[i], bank(base + i))
            for kh in range(2):
                wv = wnext("pg")
                for i in range(4):
                    for kc in range(8):
                        B.mm(bank(base + i), hT[:, kh * 8 + kc, i * 128:(i + 1) * 128], wv[:, kc, :],
                             kh == 0 and kc == 0, kh == 1 and kc == 7)
            for i in range(4):
                B.act(tgt[i], bank(base + i), AF.Tanh, scale=0.5)
                B.stt("dve", tgt[i], tgt[i], 1.0, ppt[i], ALU.add, ALU.mult)
                xr = xres[:, i, cb * 512:(cb + 1) * 512]
                B.stt("dve", xr, tgt[i], 0.5, xr, ALU.mult, ALU.add)

        gfin = cv(A, [128, D], F32)
        ost = [cv(A + 8192 + 8192 * i, [128, D], F32) for i in range(2)]
        B.dma("sp", "gf", gfin, g_final[0:1, :].partition_broadcast(128))
        B.memset("dve", ssn[:, 0:4], 0.0)
        for i in range(4):
            B.act(ost[i % 2], xres[:, i, :], AF.Square, accum=ssn[:, i:i + 1])
        rstd(ssn[:, 4:8], ssn[:, 0:4], 1.0 / D, sc1[:, 0:4])
        for i in range(4):
            B.stt("dve", ost[i % 2], xres[:, i, :], ssn[:, 4 + i:5 + i], gfin, ALU.mult, ALU.mult)
            B.dma("sp", f"o{i % 2}", out[tok0 + i * 128:tok0 + (i + 1) * 128, :], ost[i % 2])

    finals = [s_ for s_ in B.dma_sems if s_ in ("o0", "o1") or s_.startswith("dbg")]
    B.emit(finals)
    return B


_INPUT_ORDER = ["x", "p", "g_mix", "w_in", "conv_w", "conv_b", "b_i", "b_f", "lam_q1", "lam_k1", "lam_q2", "lam_k2",
                "da_sub_g", "ml_norm_g", "w_pa", "w_pb", "w_o", "g_mlp", "w_up", "w_down", "g_ple", "w_ple_gate",
                "w_ple_proj", "g_final"]


def make_in_maps(inputs, n=8):
    f = lambda a: np.ascontiguousarray(np.asarray(a, dtype=np.float32))
    shared = {}
    for k in _INPUT_ORDER:
        if k in ("x", "p"):
            continue
        a = f(inputs[k])
        if k == "g_final":
            a = a.reshape(1, D)
        else:
            a = a[0]
            if a.ndim == 1:
                a = a.reshape(1, -1)
        shared[k] = a
    xs_ = f(inputs["x"])
    ps_ = f(inputs["p"])
    maps = []
    for c in range(n):
        m = dict(shared)
        m["x"] = xs_[c]
        m["p"] = ps_[0, c]
        maps.append(m)
    return maps


def kernel(**inputs):
    B = build()
    in_maps = make_in_maps(inputs, 8)
    res = run_bass_kernel_spmd(B.nc, in_maps, core_ids=list(range(8)))
    return np.stack([np.asarray(r["out"], dtype=np.float32) for r in res.results], axis=0)
```

```python
import math
from contextlib import ExitStack
import numpy as np
import concourse.bass as bass
import concourse.mybir as mybir
from concourse.bass_utils import run_bass_kernel_spmd

F32 = mybir.dt.float32
BF16 = mybir.dt.bfloat16
U8 = mybir.dt.uint8
AF = mybir.ActivationFunctionType
ALU = mybir.AluOpType
AX = mybir.AxisListType
DTS = {F32: 4, BF16: 2, U8: 1}

S = 2048
D = 2048
T = 512
NT = S // T
KC = D // 128
EPS = 1e-6
LAMBDA_INIT = 0.8 - 0.6 * math.exp(0.0)
IN_COLS = 11272
C_AQ, C_AK, C_AV, C_MQ, C_MK, C_MV, C_MO, C_MI, C_MF, C_GA, C_GB = (
    0, 1024, 2048, 3072, 4096, 5120, 6144, 7168, 7172, 7176, 9224)
G = 128
ENGS = ("pe", "act", "dve", "pool", "sp")


class Tracker:
    def __init__(self):
        self.ops = []
        self.eng_ops = {e: [] for e in ENGS}
        self.count = {}
        self.gw = {}
        self.gr = {}
        self.seen = {e: {} for e in ENGS}
        self.sig = {e: set() for e in ENGS}

    @staticmethod
    def grans(ap):
        t = ap.tensor
        cn = t.__class__.__name__
        if cn.startswith("DRam"):
            return ()
        es = DTS[ap.dtype]
        dims = ap.ap
        pstep = dims[0][0]
        off = ap.offset % pstep if pstep else ap.offset
        span = 1 + sum((c - 1) * abs(s) for s, c in dims[1:])
        lo = off * es
        hi = (off + span) * es
        if not cn.startswith("SB"):
            return [(t.name, 0)]
        return [("S", g) for g in range(lo // G, (hi - 1) // G + 1)]

    def add(self, eng, fn, reads=(), writes=(), dma=None):
        key = eng if dma is None else "dma:" + dma
        idx = self.count.get(key, 0) + 1
        self.count[key] = idx
        deps = {}

        def dep(k, i, raw):
            if k == key and dma is None:
                if eng == "pe":
                    return
            if deps.get(k, 0) < i:
                deps[k] = i

        rg = set()
        for ap in reads:
            rg.update(self.grans(ap))
        wg = set()
        for ap in writes:
            wg.update(self.grans(ap))
        for g in rg:
            w = self.gw.get(g)
            if w:
                dep(w[0], w[1], True)
            if g[0] != "S":
                r = self.gr.get(g)
                if r:
                    for k, i in r.items():
                        if k != key:
                            dep(k, i, False)
        for g in wg:
            w = self.gw.get(g)
            if w:
                dep(w[0], w[1], False)
            r = self.gr.get(g)
            if r:
                for k, i in r.items():
                    dep(k, i, False)
        waits = []
        seen = self.seen[eng]
        for k, i in deps.items():
            if seen.get(k, 0) >= i:
                continue
            seen[k] = i
            waits.append((k, i))
            if not k.startswith("dma:"):
                self.sig[k].add(i)
        for g in rg:
            self.gr.setdefault(g, {})[key] = idx
        for g in wg:
            self.gw[g] = (key, idx)
            self.gr[g] = {}
        self.ops.append((eng, fn, waits, key, idx, dma))
        self.eng_ops[eng].append(len(self.ops) - 1)


class Builder:
    def __init__(self, debug=None):
        self.debug = debug
        self.nc = bass.Bass("TRN2", target_bir_lowering=False)
        self.tr = Tracker()
        self.es = ExitStack()
        self.dma_sems = {}
        self._rr = 0

    def op(self, eng, fn, reads=(), writes=(), dma=None):
        self.tr.add(eng, fn, reads, writes, dma)

    def act(self, out, in_, func, bias=None, scale=None, accum=None, extra_r=()):
        kw = {}
        rd = [in_] + list(extra_r)
        if bias is not None:
            kw["bias"] = bias
            if not isinstance(bias, float):
                rd.append(bias)
        if scale is not None:
            kw["scale"] = scale
            if not isinstance(scale, (float, int)):
                rd.append(scale)
        wr = [out]
        if accum is not None:
            kw["accum_out"] = accum
            wr.append(accum)
        self.op("act", lambda e: e.activation(out=out, in_=in_, func=func, **kw), rd, wr)

    def ts(self, eng, out, in0, s1, op0, s2=None, op1=None, accum=None):
        rd = [in0]
        if not isinstance(s1, (float, int)):
            rd.append(s1)
        if s2 is not None and not isinstance(s2, (float, int)):
            rd.append(s2)
        kw = {}
        if op1 is not None:
            kw["op1"] = op1
        wr = [out]
        if accum is not None:
            kw["accum_out"] = accum
            wr.append(accum)
        self.op(eng, lambda e: e.tensor_scalar(out=out, in0=in0, scalar1=s1, scalar2=s2, op0=op0, **kw), rd, wr)

    def tt(self, eng, out, in0, in1, op):
        self.op(eng, lambda e: e.tensor_tensor(out=out, in0=in0, in1=in1, op=op), [in0, in1], [out])

    def stt(self, eng, out, in0, scalar, in1, op0, op1, accum=None):
        rd = [in0, in1]
        if not isinstance(scalar, (float, int)):
            rd.append(scalar)
        kw = {}
        wr = [out]
        if accum is not None:
            kw["accum_out"] = accum
            wr.append(accum)
        self.op(eng, lambda e: e.scalar_tensor_tensor(out=out, in0=in0, scalar=scalar, in1=in1, op0=op0, op1=op1, **kw), rd, wr)

    def copy(self, eng, out, in_):
        if eng == "act":
            self.act(out, in_, AF.Copy)
        else:
            self.op(eng, lambda e: e.tensor_copy(out=out, in_=in_), [in_], [out])

    def evac(self, out, in_):
        self._rr ^= 1
        self.copy("act" if self._rr else "dve", out, in_)

    def memset(self, eng, ap, val):
        self.op(eng, lambda e: e.memset(ap, val), [], [ap])

    def mm(self, out, lhsT, rhs, start, stop, skip=False):
        kw = {"skip_group_check": True} if skip else {}
        self.op("pe", lambda e: e.matmul(out, lhsT=lhsT, rhs=rhs, start=start, stop=stop, **kw), [lhsT, rhs], [out])

    def tp(self, out, in_, ident):
        self.op("pe", lambda e: e.transpose(out=out, in_=in_, identity=ident), [in_, ident], [out])

    def dma(self, queue, sem, out, in_, nc_ok=False):
        if sem not in self.dma_sems:
            self.dma_sems[sem] = self.es.enter_context(self.nc.semaphore("d_" + sem))
        kw = {"allow_slow_non_contiguous": True} if nc_ok else {}
        self.op(queue, lambda e: e.dma_start(out=out, in_=in_, **kw), [in_], [out], dma=sem)

    def carve(self, off, shape, dt, p0=0):
        n = 1
        for s_ in shape[1:]:
            n *= s_
        nb = n * DTS[dt]
        assert off % 4 == 0 and off + nb <= self.big_bytes, (off, nb, self.big_bytes)
        ap = self.big[p0:p0 + shape[0], off:off + nb].bitcast(dt)
        if len(shape) == 3:
            ap = ap.rearrange("p (a b) -> p a b", a=shape[1], b=shape[2])
        elif len(shape) == 4:
            ap = ap.rearrange("p (a b c) -> p a b c", a=shape[1], b=shape[2], c=shape[3])
        return ap

    def emit(self, final_waits):
        nc = self.nc
        tr = self.tr
        sems = {e: self.es.enter_context(nc.semaphore("e_" + e)) for e in ENGS}
        rank = {}
        for e in ENGS:
            for r, i in enumerate(sorted(tr.sig[e])):
                rank[(e, i)] = r + 1
        block = self.es.enter_context(nc.Block())

        def run(eng_name):
            def body(e):
                for oi in tr.eng_ops[eng_name]:
                    _, fn, waits, key, idx, dma = tr.ops[oi]
                    for k, i in waits:
                        if k.startswith("dma:"):
                            e.wait_ge(self.dma_sems[k[4:]], 16 * i)
                        else:
                            e.wait_ge(sems[k], rank[(k, i)])
                    ins = fn(e)
                    if dma is not None:
                        ins.then_inc(self.dma_sems[dma], 16)
                    elif idx in tr.sig[eng_name]:
                        ins.then_inc(sems[eng_name], 1)
                if eng_name == "sp":
                    for sname in final_waits:
                        e.wait_ge(self.dma_sems[sname], 16 * tr.count["dma:" + sname])
            return body

        block.tensor(run("pe"))
        block.scalar(run("act"))
        block.vector(run("dve"))
        block.gpsimd(run("pool"))
        block.sync(run("sp"))


def build(debug=None):
    B = Builder(debug)
    nc = B.nc
    es = B.es
    dram = {}

    def din(name, shape):
        dram[name] = nc.dram_tensor(name, list(shape), F32, kind="ExternalInput").ap()
        return dram[name]

    x = din("x", (S, D))
    p = din("p", (S, 256))
    g_mix = din("g_mix", (1, D))
    w_in = din("w_in", (D, IN_COLS))
    conv_w = din("conv_w", (4, 2048))
    conv_b = din("conv_b", (1, 2048))
    b_i = din("b_i", (1, 4))
    b_f = din("b_f", (1, 4))
    lam_q1 = din("lam_q1", (1, 64))
    lam_k1 = din("lam_k1", (1, 64))
    lam_q2 = din("lam_q2", (1, 64))
    lam_k2 = din("lam_k2", (1, 64))
    da_sub_g = din("da_sub_g", (1, 128))
    ml_norm_g = din("ml_norm_g", (1, 1024))
    w_pa = din("w_pa", (1024, D))
    w_pb = din("w_pb", (1024, D))
    w_o = din("w_o", (D, D))
    g_mlp = din("g_mlp", (1, D))
    w_up = din("w_up", (D, 4 * D))
    w_down = din("w_down", (4 * D, D))
    g_ple = din("g_ple", (1, D))
    w_ple_gate = din("w_ple_gate", (D, D))
    w_ple_proj = din("w_ple_proj", (256, D))
    g_final = din("g_final", (1, D))
    out = nc.dram_tensor("out", [S, D], F32, kind="ExternalOutput").ap()
    dbg = None
    if debug:
        dbg = nc.dram_tensor("dbg", list(debug[1]), F32, kind="ExternalOutput").ap()

    al = lambda v: (v + G - 1) // G * G
    O_KT = 0
    O_V1 = al(O_KT + 32768)
    O_CST = al(O_V1 + 33024)
    O_CGB = al(O_CST + 8224)
    O_MLG = al(O_CGB + 4112)
    O_SM = al(O_MLG + 4096)
    O_W = al(O_SM + 8192)
    O_H = al(O_W + 32768)
    O_A = al(O_H + 16384)
    ARENA = 69632
    B.big_bytes = O_A + ARENA
    B.big = es.enter_context(nc.sbuf_tensor("big", [128, B.big_bytes], U8))
    cv = B.carve

    KT = cv(O_KT, [128, 8, S], BF16)
    V1 = cv(O_V1, [128, 16, 8, 129], BF16)
    Cst = cv(O_CST, [128, 4, 2, 257], F32)
    Cgb = cv(O_CGB, [128, 4, 2, 257], BF16)
    mlgc = cv(O_MLG, [128, 8], F32)
    hT = cv(O_H, [128, KC, T], BF16)
    wslots = [cv(O_W + 8192 * i, [128, 4096], BF16) for i in range(4)]

    sm = [O_SM]

    def small(shape, dt=F32, align=128):
        n = 1
        for s_ in shape[1:]:
            n *= s_
        nb = (n * DTS[dt] + align - 1) // align * align
        o = sm[0]
        sm[0] += nb
        assert sm[0] <= O_SM + 8192
        return cv(o, shape, dt)

    ident_b = small([128, 128], BF16)
    ident_f = small([128, 128], F32)
    mask01 = small([64, 64], F32)
    subg = small([128, 128], F32)
    lamt = small([128, 4, 64], F32)
    lams = small([128, 8], F32)
    convw = small([128, 16, 4], F32)
    convb = small([128, 16], F32)
    gmix = small([128, 16], F32)
    gmlp = small([128, 16], F32)
    gple = small([128, 16], F32)
    bif = small([4, 4], F32)
    ones4 = small([4, 128], F32)
    cst = small([128, 4], F32)
    halo = small([128, 16, 3], F32)
    alpha_tok = small([64, 8, 4], F32)
    eb_tok = small([64, 8, 4], F32)
    gam_bc = small([128, 4, 8], F32)
    mp = small([4, 16], F32)
    ssn = small([128, 8], F32)
    ssa = small([128, 2, 32], F32)
    ssm = small([64, 3, 8], F32)
    sc1 = small([128, 8], F32)
    gsm = small([4, 6, 8], F32)

    psb = [es.enter_context(nc.psum_tensor(f"pb{i}", [128, 512], F32)) for i in range(8)]

    def bank(i):
        return psb[i][:, :]

    def bank_bf(i):
        return psb[i][:, :].bitcast(BF16)

    wstate = {"sched": [], "issued": 0, "next": 0}

    def wsched_add(tag, src, kcn, ncols):
        wstate["sched"].append((tag, src, kcn, ncols))

    def wview(k):
        tag, src, kcn, ncols = wstate["sched"][k]
        return wslots[k % 4][:, 0:kcn * ncols].rearrange("p (a b) -> p a b", a=kcn, b=ncols)

    def wnext(tag):
        k = wstate["next"]
        assert wstate["sched"][k][0] == tag, (wstate["sched"][k][0], tag)
        while wstate["issued"] < min(len(wstate["sched"]), k + 4):
            j = wstate["issued"]
            _, src, kcn, ncols = wstate["sched"][j]
            B.dma("pool", f"w{j % 4}", wview(j), src.rearrange("(kc p) n -> p kc n", p=128))
            wstate["issued"] += 1
        wstate["next"] += 1
        return wview(k)

    for t in range(NT):
        wsched_add("gif", w_in[:, C_MI:C_MI + 8], 16, 8)
        for b_ in range(4):
            wsched_add("ak", w_in[:, C_AK + 256 * b_:C_AK + 256 * (b_ + 1)], 16, 256)
        for cb in range(2):
            for kh in range(2):
                wsched_add("av", w_in[kh * 1024:(kh + 1) * 1024, C_AV + 512 * cb:C_AV + 512 * (cb + 1)], 8, 512)
        for b_ in range(4):
            wsched_add("aq", w_in[:, C_AQ + 256 * b_:C_AQ + 256 * (b_ + 1)], 16, 256)
        for h in range(4):
            for c0 in (C_MQ, C_MK, C_MV, C_MO):
                wsched_add("ml", w_in[:, c0 + 256 * h:c0 + 256 * (h + 1)], 16, 256)
        for ng in range(8):
            wsched_add("ga", w_in[:, C_GA + 256 * ng:C_GA + 256 * (ng + 1)], 16, 256)
            wsched_add("gb", w_in[:, C_GB + 256 * ng:C_GB + 256 * (ng + 1)], 16, 256)
            wsched_add("pa", w_pa[:, 256 * ng:256 * (ng + 1)], 8, 256)
            wsched_add("pb", w_pb[:, 256 * ng:256 * (ng + 1)], 8, 256)
        for cb in range(4):
            for kh in range(2):
                wsched_add("wo", w_o[kh * 1024:(kh + 1) * 1024, 512 * cb:512 * (cb + 1)], 8, 512)
        for hf in range(2):
            for b_ in range(16):
                c0 = hf * 4096 + 256 * b_
                wsched_add("up", w_up[:, c0:c0 + 256], 16, 256)
            for cb in range(4):
                for kb in range(4):
                    r0 = hf * 4096 + kb * 1024
                    wsched_add("dn", w_down[r0:r0 + 1024, 512 * cb:512 * (cb + 1)], 8, 512)
        for cb in range(4):
            wsched_add("pp", w_ple_proj[:, 512 * cb:512 * (cb + 1)], 2, 512)
            for kh in range(2):
                wsched_add("pg", w_ple_gate[kh * 1024:(kh + 1) * 1024, 512 * cb:512 * (cb + 1)], 8, 512)

    B.memset("pool", ident_b, 0.0)
    B.op("pool", lambda e: e.affine_select(out=ident_b, in_=ident_b, pattern=[[-1, 128]], compare_op=ALU.not_equal,
                                           fill=1.0, base=0, channel_multiplier=1), [ident_b], [ident_b])
    B.memset("pool", ident_f, 0.0)
    B.op("pool", lambda e: e.affine_select(out=ident_f, in_=ident_f, pattern=[[-1, 128]], compare_op=ALU.not_equal,
                                           fill=1.0, base=0, channel_multiplier=1), [ident_f], [ident_f])
    B.memset("pool", mask01, 1.0)
    B.op("pool", lambda e: e.affine_select(out=mask01, in_=mask01, pattern=[[1, 64]], compare_op=ALU.is_ge,
                                           fill=0.0, base=0, channel_multiplier=-1), [mask01], [mask01])
    B.memset("pool", ones4, 1.0)
    B.memset("pool", cst[:, 0:1], EPS)
    B.memset("pool", cst[:, 1:2], 1.0)
    B.memset("pool", halo, 0.0)
    B.memset("pool", mp, 0.0)
    B.memset("pool", Cst, 0.0)
    B.memset("pool", V1[:, :, :, 128:129], 1.0)

    B.dma("sp", "c0", subg, da_sub_g[0:1, :].partition_broadcast(128))
    for i_, l_ in enumerate((lam_q1, lam_k1, lam_q2, lam_k2)):
        B.dma("sp", f"c{1 + i_}", lamt[:, i_, :], l_[0:1, :].partition_broadcast(128))
    for j_ in range(4):
        B.dma("sp", f"c5{j_}", convw[:, :, j_], conv_w[j_, :].rearrange("(kc p) -> p kc", p=128), nc_ok=True)
    B.dma("sp", "c6", convb, conv_b[0, :].rearrange("(kc p) -> p kc", p=128), nc_ok=True)
    B.dma("sp", "c7", gmix, g_mix[0, :].rearrange("(kc p) -> p kc", p=128), nc_ok=True)
    B.dma("sp", "c8", gmlp, g_mlp[0, :].rearrange("(kc p) -> p kc", p=128), nc_ok=True)
    B.dma("sp", "c9", gple, g_ple[0, :].rearrange("(kc p) -> p kc", p=128), nc_ok=True)
    B.dma("sp", "c10", bif[:, 0:1], b_i.rearrange("o h -> h o"), nc_ok=True)
    B.dma("sp", "c11", bif[:, 1:2], b_f.rearrange("o h -> h o"), nc_ok=True)
    B.dma("sp", "c12", mlgc, ml_norm_g[0, :].rearrange("(kc p) -> p kc", p=128), nc_ok=True)
    B.ts("dve", subg, subg, 1.0 - LAMBDA_INIT, ALU.mult)
    B.ts("dve", mlgc, mlgc, 0.5, ALU.mult)
    B.ts("dve", bif[:, 2:3], bif[:, 1:2], -1.0, ALU.mult)
    B.ts("dve", convw[:, 0:8, :], convw[:, 0:8, :], 1.0 / 32.0, ALU.mult)
    B.ts("dve", convw[:, 8:16, :], convw[:, 8:16, :], 0.5, ALU.mult)
    B.ts("dve", convb[:, 0:8], convb[:, 0:8], 1.0 / 32.0, ALU.mult)
    B.ts("dve", convb[:, 8:16], convb[:, 8:16], 0.5, ALU.mult)
    B.memset("dve", lams, 0.0)
    B.tt("dve", lamt[:, 0, :], lamt[:, 0, :], lamt[:, 1, :], ALU.mult)
    B.tt("dve", lamt[:, 2, :], lamt[:, 2, :], lamt[:, 3, :], ALU.mult)
    B.op("dve", lambda e: e.reduce_sum(out=lams[:, 0:1], in_=lamt[:, 0, :], axis=AX.X), [lamt[:, 0, :]], [lams[:, 0:1]])
    B.op("dve", lambda e: e.reduce_sum(out=lams[:, 1:2], in_=lamt[:, 2, :], axis=AX.X), [lamt[:, 2, :]], [lams[:, 1:2]])
    B.act(lams[:, 2:4], lams[:, 0:2], AF.Exp)
    B.tt("dve", lams[:, 4:5], lams[:, 3:4], lams[:, 2:3], ALU.subtract)
    B.ts("dve", lams[:, 4:5], lams[:, 4:5], -LAMBDA_INIT, ALU.add)
    neglam = lams[:, 4:5]

    def rstd(dst, src, inv_n, tmp):
        B.act(tmp, src, AF.Ln, bias=cst[0:src.shape[0], 0:1], scale=inv_n)
        B.act(dst, tmp, AF.Exp, scale=-0.5)

    def norm_to_hT(srcs, gcol, xs):
        B.memset("dve", ssn[:, 0:4], 0.0)
        for i, s_ in enumerate(srcs):
            B.act(xs[:, i, :], s_, AF.Square, accum=ssn[:, i:i + 1])
        rstd(ssn[:, 4:8], ssn[:, 0:4], 1.0 / D, sc1[:, 0:4])
        for i, s_ in enumerate(srcs):
            if i % 2 == 0:
                B.act(xs[:, i, :], s_, AF.Copy, scale=ssn[:, 4 + i:5 + i])
            else:
                B.ts("dve", xs[:, i, :], s_, ssn[:, 4 + i:5 + i], ALU.mult)
        for kc in range(KC):
            pb_ = 6 + (kc % 2)
            for i in range(4):
                B.tp(bank_bf(pb_)[:, i * 128:(i + 1) * 128], xs[:, i, kc * 128:(kc + 1) * 128], ident_b)
            if kc % 2 == 0:
                B.ts("dve", hT[:, kc, :], bank_bf(pb_)[:, 0:512], gcol[:, kc:kc + 1], ALU.mult)
            else:
                B.act(hT[:, kc, :], bank_bf(pb_)[:, 0:512], AF.Copy, scale=gcol[:, kc:kc + 1])

    dense_rr = [0]

    def dbank(nb=6):
        dense_rr[0] = (dense_rr[0] + 1) % nb
        return dense_rr[0]

    def ws_chunk(wv, n, rhs_of_kc, nk, N=T):
        pb_ = dbank()
        o = bank(pb_)[:, 0:N]
        for kc in range(nk):
            B.mm(o, wv[:, kc, n * 128:(n + 1) * 128], rhs_of_kc(kc), kc == 0, kc == nk - 1)
        return o

    def dump(ap, r0, c0):
        P_, F_ = ap.shape[0], ap.shape[1]
        dump.n = getattr(dump, "n", 0) + 1
        B.dma("sp", f"dbg{dump.n}", dbg[r0:r0 + P_, c0:c0 + F_], ap)

    for t in range(NT):
        tok0 = t * T
        A = O_A
        xs = cv(A, [128, 4, D], BF16)
        xld = [cv(A + 16384 + 8192 * i, [128, D], F32) for i in range(2)]
        B.memset("dve", ssn[:, 0:4], 0.0)
        for i in range(4):
            xb_ = xld[i % 2]
            B.dma("sp", f"x{i % 2}", xb_, x[tok0 + i * 128:tok0 + (i + 1) * 128, :])
            B.act(xs[:, i, :], xb_, AF.Square, accum=ssn[:, i:i + 1])
        rstd(ssn[:, 4:8], ssn[:, 0:4], 1.0 / D, sc1[:, 0:4])
        for i in range(4):
            xb_ = xld[i % 2]
            B.dma("sp", f"x{i % 2}", xb_, x[tok0 + i * 128:tok0 + (i + 1) * 128, :])
            if i % 2 == 0:
                B.act(xs[:, i, :], xb_, AF.Copy, scale=ssn[:, 4 + i:5 + i])
            else:
                B.ts("dve", xs[:, i, :], xb_, ssn[:, 4 + i:5 + i], ALU.mult)
        for kc in range(KC):
            pb_ = 6 + (kc % 2)
            for i in range(4):
                B.tp(bank_bf(pb_)[:, i * 128:(i + 1) * 128], xs[:, i, kc * 128:(kc + 1) * 128], ident_b)
            if kc % 2 == 0:
                B.ts("dve", hT[:, kc, :], bank_bf(pb_)[:, 0:512], gmix[:, kc:kc + 1], ALU.mult)
            else:
                B.act(hT[:, kc, :], bank_bf(pb_)[:, 0:512], AF.Copy, scale=gmix[:, kc:kc + 1])
        if debug and debug[0] == "hT" and t == 0:
            tmpf = cv(A + 32768, [128, T], F32)
            for kc in range(KC):
                B.copy("dve", tmpf, hT[:, kc, :])
                dump(tmpf, kc * 128, 0)
            break

        GA_ = A + 32768
        wv = wnext("gif")
        for gi in range(2):
            o = bank(gi)[0:4, 0:T]
            for kc in range(KC):
                B.mm(o, wv[:, kc, 4 * gi:4 * gi + 4], hT[:, kc, :], kc == 0, kc == KC - 1)
        li = cv(GA_, [4, 8, 64], F32)
        cs0 = cv(GA_ + 2048, [4, 8, 64], F32)
        cs1 = cv(GA_ + 4096, [4, 8, 64], F32)
        av_ = cv(GA_ + 6144, [4, 8, 64], F32)
        alph = cv(GA_ + 8192, [4, 8, 64], F32)
        ebf = cv(GA_ + 10240, [4, 8, 64], F32)
        Gm = cv(GA_ + 12288, [4, 4, 8], F32)
        pgi = bank(0)[0:4, 0:T].rearrange("p (c l) -> p c l", c=8, l=64)
        pgf = bank(1)[0:4, 0:T].rearrange("p (c l) -> p c l", c=8, l=64)
        B.ts("dve", li, pgi, bif[:, 0:1], ALU.add)
        B.act(cs1, pgf, AF.Exp, bias=bif[:, 2:3], scale=-1.0)
        B.act(cs0, cs1, AF.Ln, bias=cst[0:4, 1:2], scale=1.0)
        X_, Y_ = cs0, cs1
        for d_ in (1, 2, 4, 8, 16, 32):
            B.copy("dve", Y_[:, :, 0:d_], X_[:, :, 0:d_])
            B.tt("dve", Y_[:, :, d_:64], X_[:, :, d_:64], X_[:, :, 0:64 - d_], ALU.add)
            X_, Y_ = Y_, X_
        csum = X_
        B.tt("dve", av_, li, csum, ALU.add)
        A_c, Bl, Mc, nMc, garg, gam = (gsm[:, k_, :] for k_ in range(6))
        B.op("dve", lambda e: e.tensor_reduce(out=A_c, in_=av_, axis=AX.X, op=ALU.max), [av_], [A_c])
        B.ts("dve", Bl, csum[:, :, 63], -1.0, ALU.mult)
        for c in range(8):
            B.tt("dve", Mc[:, c:c + 1], mp[:, c:c + 1], A_c[:, c:c + 1], ALU.max)
            B.tt("dve", mp[:, c + 1:c + 2], Bl[:, c:c + 1], Mc[:, c:c + 1], ALU.add)
        B.tt("dve", garg, mp[:, 0:8], Mc, ALU.subtract)
        B.ts("dve", nMc, Mc, -1.0, ALU.mult)
        B.act(gam, garg, AF.Exp)
        for c in range(8):
            B.act(alph[:, c, :], av_[:, c, :], AF.Exp, bias=nMc[:, c:c + 1], scale=1.0)
            B.act(ebf[:, c, :], csum[:, c, :], AF.Exp, bias=nMc[:, c:c + 1], scale=1.0)
        B.copy("dve", mp[:, 0:1], mp[:, 8:9])
        for c in range(8):
            B.tp(bank(2)[0:64, c * 4:(c + 1) * 4], alph[:, c, :], ident_f[0:4, 0:4])
            B.tp(bank(3)[0:64, c * 4:(c + 1) * 4], ebf[:, c, :], ident_f[0:4, 0:4])
        B.copy("dve", alpha_tok, bank(2)[0:64, 0:32].rearrange("p (c h) -> p c h", c=8, h=4))
        B.copy("dve", eb_tok, bank(3)[0:64, 0:32].rearrange("p (c h) -> p c h", c=8, h=4))
        for h in range(4):
            B.ts("dve", Gm[:, h, :], gam, ident_f[0:4, h:h + 1], ALU.mult)
        B.mm(bank(4)[:, 0:32], ones4, Gm.rearrange("p a b -> p (a b)"), True, True)
        B.copy("dve", gam_bc, bank(4)[:, 0:32].rearrange("p (a b) -> p a b", a=4, b=8))

        if debug and debug[0] == "gates" and t == debug[2]:
            dump(gam_bc.rearrange("p a b -> p (a b)"), 0, 0)
            dump(alpha_tok.rearrange("p a b -> p (a b)"), 128, 0)
            dump(eb_tok.rearrange("p a b -> p (a b)"), 192, 0)
            dump(mp, 256, 0)
            dump(gsm.rearrange("p a b -> p (a b)"), 260, 0)
            break
        QT = cv(A + 45312, [128, 8, T], BF16)
        for b_ in range(4):
            wv = wnext("ak")
            for n in range(2):
                o = ws_chunk(wv, n, lambda kc: hT[:, kc, :], KC)
                B.evac(KT[:, 2 * b_ + n, tok0:tok0 + T], o)
        for cb in range(2):
            accs = [bank(i) for i in range(4)]
            for kh in range(2):
                wv = wnext("av")
                for i in range(4):
                    for kc in range(8):
                        B.mm(accs[i], hT[:, kh * 8 + kc, i * 128:(i + 1) * 128], wv[:, kc, :],
                             kh == 0 and kc == 0, kh == 1 and kc == 7)
            for i in range(4):
                B.evac(V1[:, 4 * t + i, 4 * cb:4 * cb + 4, 0:128], accs[i].rearrange("p (h d) -> p h d", h=4, d=128))
        for b_ in range(4):
            wv = wnext("aq")
            for n in range(2):
                o = ws_chunk(wv, n, lambda kc: hT[:, kc, :], KC)
                B.evac(QT[:, 2 * b_ + n, :], o)

        ya_tok = cv(A + 45312, [128, 4, 1024], BF16)
        yaT = cv(A + 16384, [128, 8, T], BF16)
        ybT = cv(A + 24576, [128, 8, T], BF16)
        ET = [cv(A + 53504 + 2048 * i, [128, 2, T], BF16) for i in range(2)]
        o_all = cv(A, [128, 4, 8, 128], F32)
        dtmp = cv(A + 57600, [128, 128], F32)
        sqj = cv(A + 58112, [128, 128], BF16)
        nkb = 4 * t + 4
        B.memset("dve", ssa[:, 0, :], 0.0)
        jj = 0
        for h in range(8):
            def s_mm(j, jj_):
                q0_ = max(0, j - 4 * t) * 128
                for m in range(2):
                    B.mm(bank((jj_ % 2) * 2 + m)[:, q0_:T], KT[64 * m:64 * m + 64, h, j * 128:(j + 1) * 128],
                         QT[64 * m:64 * m + 64, h, q0_:T], True, True)
            s_mm(0, jj)
            for j in range(nkb):
                qi0 = max(0, j - 4 * t)
                q0 = qi0 * 128
                eb_ = ET[jj % 2]
                sb0 = (jj % 2) * 2
                for m in range(2):
                    B.act(eb_[:, m, q0:T], bank(sb0 + m)[:, q0:T], AF.Exp, scale=0.125)
                if j >= 4 * t:
                    B.memset("dve", eb_[64:128, :, q0:q0 + 64], 0.0)
                if j + 1 < nkb:
                    s_mm(j + 1, jj + 1)
                for i in range(qi0, 4):
                    for m in range(2):
                        B.mm(bank(4 + i)[:, m * 129:(m + 1) * 129], eb_[:, m, i * 128:(i + 1) * 128],
                             V1[:, j, h, :], j == 0 and m == 0, j == 4 * t + i and m == 1, skip=True)
                jj += 1
            for i in range(4):
                acc = bank(4 + i)[:, 0:258].rearrange("p (m d) -> p m d", m=2, d=129)
                rs = sc1[:, 0:2]
                B.op("dve", lambda e, acc=acc, rs=rs: e.reciprocal(out=rs, in_=acc[:, :, 128]), [acc[:, :, 128]], [rs])
                B.tt("dve", sc1[:, 2:3], sc1[:, 1:2], neglam, ALU.mult)
                B.ts("dve", dtmp, acc[:, 0, 0:128], sc1[:, 0:1], ALU.mult)
                B.stt("dve", o_all[:, i, h, :], acc[:, 1, 0:128], sc1[:, 2:3], dtmp, ALU.mult, ALU.add)
                B.act(sqj, o_all[:, i, h, :], AF.Square, accum=ssa[:, 0, i * 8 + h:i * 8 + h + 1])
        rstd(ssa[:, 1, :], ssa[:, 0, :], 1.0 / 128.0, ssa[:, 0, :])
        for i in range(4):
            for h in range(8):
                B.stt("dve", ya_tok[:, i, h * 128:(h + 1) * 128], o_all[:, i, h, :], ssa[:, 1, i * 8 + h:i * 8 + h + 1],
                      subg, ALU.mult, ALU.mult)
        for kc in range(8):
            pb_ = 6 + (kc % 2)
            for i in range(4):
                B.tp(bank_bf(pb_)[:, i * 128:(i + 1) * 128], ya_tok[:, i, kc * 128:(kc + 1) * 128], ident_b)
            B.evac(yaT[:, kc, :], bank_bf(pb_)[:, 0:512])
        if debug and debug[0] == "yaT" and t == debug[2]:
            tmpf = cv(A + 32768, [128, T], F32)
            for kc in range(8):
                B.copy("dve", tmpf, yaT[:, kc, :])
                dump(tmpf, kc * 128, 0)
            break

        M0 = A + 32768
        raw = [cv(M0 + 2176 * i, [128, 515], F32) for i in range(2)]
        accb = [cv(M0 + 4352 + 2048 * i, [128, T], F32) for i in range(2)]
        thb = [cv(M0 + 8448 + 2048 * i, [128, T], F32) for i in range(2)]
        mqT = cv(M0 + 12544, [128, 2, T], BF16)
        mkT = cv(M0 + 14592, [128, 2, T], BF16)
        stg = cv(M0 + 16640, [128, 2, T], BF16)
        mk_tok = cv(M0 + 18688, [64, 8, 256], BF16)
        vpr = cv(M0 + 22784, [64, 8, 257], BF16)
        tho = cv(M0 + 27008, [64, 8, 256], BF16)
        yb = cv(A, [64, 8, 256], BF16)
        Pm = cv(A + 4096, [64, 8, 64], BF16)
        g2 = cv(A + 5120, [64, 256], F32)
        hm = cv(A + 6144, [64, 8, 256], F32)
        Cg2 = cv(O_CGB, [128, 2, 2, 257], BF16)
        def tok_transposes(src, n):
            pv = bank_bf(6 + n)[0:64, :].rearrange("p (c d) -> p c d", c=8, d=128)
            for c in range(8):
                B.tp(pv[:, c, :], src[:, n, c * 64:(c + 1) * 64], ident_b)
            return pv

        def prologue_qk(h):
            for kind in range(2):
                wv = wnext("ml")
                for n in range(2):
                    o = ws_chunk(wv, n, lambda kc: hT[:, kc, :], KC)
                    kcg = kind * 8 + h * 2 + n
                    rw = raw[n]
                    ac = accb[n]
                    B.copy("dve", rw[:, 0:3], halo[:, kcg, :])
                    B.copy("act", rw[:, 3:515], o)
                    B.copy("dve", halo[:, kcg, :], rw[:, 512:515])
                    B.act(ac, rw[:, 3:515], AF.Identity, bias=convb[:, kcg:kcg + 1], scale=convw[:, kcg, 3:4])
                    for j_ in (2, 1, 0):
                        B.stt("dve", ac, rw[:, j_:j_ + T], convw[:, kcg, j_:j_ + 1], ac, ALU.mult, ALU.add)
                    B.act(thb[n], ac, AF.Tanh, scale=16.0 if kind == 0 else 1.0)
                    dst = mqT if kind == 0 else mkT
                    B.stt("dve", dst[:, n, :], thb[n], 1.0, ac, ALU.add, ALU.mult)

        def mk_to_tok():
            for n in range(2):
                pv = tok_transposes(mkT, n)
                B.evac(mk_tok[:, :, n * 128:(n + 1) * 128], pv)

        def prologue_vo(h):
            for kind in (2, 3):
                wv = wnext("ml")
                for n in range(2):
                    o = ws_chunk(wv, n, lambda kc: hT[:, kc, :], KC)
                    if kind == 2:
                        B.evac(stg[:, n, :], o)
                    else:
                        B.act(stg[:, n, :], o, AF.Tanh, scale=0.5)
                for n in range(2):
                    pv = tok_transposes(stg, n)
                    if kind == 2:
                        for c in range(8):
                            B.ts("dve", vpr[:, c, n * 128:(n + 1) * 128], pv[:, c, :], alpha_tok[:, c, h:h + 1], ALU.mult)
                    else:
                        B.evac(tho[:, :, n * 128:(n + 1) * 128], pv)
                if kind == 2:
                    B.copy("dve", vpr[:, :, 256], alpha_tok[:, :, h])

        def chunk_loop(h):
            B.memset("dve", ssm[:, 0, :], 0.0)
            psPa = bank(0)[0:64, :].rearrange("p (c l) -> p c l", c=8, l=64)
            for c in range(8):
                cs = slice(c * 64, (c + 1) * 64)
                for n in range(2):
                    B.mm(psPa[:, c, :], mkT[:, n, cs], mqT[:, n, cs], n == 0, n == 1)
            for c in range(8):
                B.tt("dve", Pm[:, c, :], psPa[:, c, :], mask01, ALU.mult)

            def finalize(c):
                psN = bank(1 + (c % 2))[0:64, 0:257]
                dm = sc1[0:64, 4:5]
                B.act(sc1[0:64, 6:7], psN[:, 256:257], AF.Abs)
                B.ts("dve", dm, sc1[0:64, 6:7], eb_tok[:, c, h:h + 1], ALU.max)
                B.op("dve", lambda e, dm=dm: e.reciprocal(out=sc1[0:64, 5:6], in_=dm), [dm], [sc1[0:64, 5:6]])
                B.act(hm[:, c, :], psN[:, 0:256], AF.Copy, scale=sc1[0:64, 5:6])
                B.act(g2, hm[:, c, :], AF.Square, accum=ssm[:, 0, c:c + 1])

            for c in range(8):
                cs = slice(c * 64, (c + 1) * 64)
                par = c % 2
                gcol = gam_bc[:, h, c:c + 1]
                psCs = [bank(3 + 2 * par + n)[:, 0:257] for n in range(2)]
                for n in range(2):
                    B.mm(psCs[n], mk_tok[:, c, n * 128:(n + 1) * 128], vpr[:, c, :], True, True)
                B.ts("dve", Cg2[:, par, :, :], Cst[:, h, :, :], gcol, ALU.mult)
                for n in range(2):
                    B.stt("dve", Cst[:, h, n, :], Cst[:, h, n, :], gcol, psCs[n], ALU.mult, ALU.add)
                psN = bank(1 + par)[0:64, 0:257]
                B.mm(psN, Pm[:, c, :], vpr[:, c, :], True, False)
                for n in range(2):
                    B.mm(psN, mqT[:, n, cs], Cg2[:, par, n, :], False, n == 1)
                if c >= 1:
                    finalize(c - 1)
            finalize(7)
            rstd(ssm[:, 1, :], ssm[:, 0, :], 1.0 / 256.0, ssm[:, 2, :])

        def epilogue(h):
            for c in range(8):
                B.act(hm[:, c, :], hm[:, c, :], AF.Copy, scale=ssm[:, 1, c:c + 1])
                B.stt("dve", yb[:, c, :], tho[:, c, :], 1.0, hm[:, c, :], ALU.add, ALU.mult)
            for n in range(2):
                pb_ = 6 + n
                for c in range(8):
                    B.tp(bank_bf(pb_)[:, c * 64:(c + 1) * 64], yb[:, c, n * 128:(n + 1) * 128], ident_b[0:64, 0:64])
                B.act(ybT[:, 2 * h + n, :], bank_bf(pb_)[:, 0:512], AF.Copy, scale=mlgc[:, 2 * h + n:2 * h + n + 1])

        for h in range(4):
            prologue_qk(h)
            if h > 0:
                epilogue(h - 1)
            prologue_vo(h)
            mk_to_tok()
            chunk_loop(h)
        epilogue(3)
        if debug and debug[0] == "ybT" and t == debug[2]:
            tmpf = cv(A + 32768, [128, T], F32)
            for kc in range(8):
                B.copy("dve", tmpf, ybT[:, kc, :])
                dump(tmpf, kc * 128, 0)
            break

        mergedT = cv(A, [128, KC, T], BF16)
        tmpA = [cv(A + 32768 + 2048 * i, [128, T], F32) for i in range(2)]
        tmpB = [cv(A + 36864 + 2048 * i, [128, T], F32) for i in range(2)]
        for ng in range(8):
            wv = wnext("ga")
            for n in range(2):
                o = ws_chunk(wv, n, lambda kc: hT[:, kc, :], KC)
                B.act(tmpA[n], o, AF.Tanh, scale=0.5)
            wv = wnext("gb")
            for n in range(2):
                o = ws_chunk(wv, n, lambda kc: hT[:, kc, :], KC)
                B.act(tmpB[n], o, AF.Tanh, scale=0.5)
            wv = wnext("pa")
            for n in range(2):
                o = ws_chunk(wv, n, lambda kc: yaT[:, kc, :], 8)
                B.stt("dve", tmpA[n], tmpA[n], 1.0, o, ALU.add, ALU.mult)
            wv = wnext("pb")
            for n in range(2):
                o = ws_chunk(wv, n, lambda kc: ybT[:, kc, :], 8)
                B.stt("dve", tmpB[n], tmpB[n], 1.0, o, ALU.add, ALU.mult)
                B.tt("dve", mergedT[:, 2 * ng + n, :], tmpA[n], tmpB[n], ALU.add)

        xres = cv(A + 32768, [128, 4, D], F32)
        xq = [cv(A + 16384 + 2048 * i, [128, 512], F32) for i in range(4)]
        xqi = 0
        for cb in range(4):
            base = 4 * (cb % 2)
            for kh in range(2):
                wv = wnext("wo")
                for i in range(4):
                    for kc in range(8):
                        B.mm(bank(base + i), mergedT[:, kh * 8 + kc, i * 128:(i + 1) * 128], wv[:, kc, :],
                             kh == 0 and kc == 0, kh == 1 and kc == 7)
            for i in range(4):
                xb_ = xq[xqi % 4]
                B.dma("sp", f"xq{xqi % 4}", xb_, x[tok0 + i * 128:tok0 + (i + 1) * 128, cb * 512:(cb + 1) * 512])
                xqi += 1
                B.stt("dve", xres[:, i, cb * 512:(cb + 1) * 512], bank(base + i), 0.5, xb_, ALU.mult, ALU.add)
        if debug and debug[0] == "x1" and t == debug[2]:
            for i in range(4):
                dump(xres[:, i, :], i * 128, 0)
            break

        xs = cv(A, [128, 4, D], BF16)
        norm_to_hT([xres[:, i, :] for i in range(4)], gmlp, xs)

        uT = cv(A, [128, 32, T], BF16)
        rt = [cv(A + 65536 + 2048 * i, [128, T], F32) for i in range(2)]
        for hf in range(2):
            for b_ in range(16):
                wv = wnext("up")
                for n in range(2):
                    o = ws_chunk(wv, n, lambda kc: hT[:, kc, :], KC)
                    r_ = rt[n]
                    B.act(r_, o, AF.Relu)
                    B.tt("dve", uT[:, 2 * b_ + n, :], r_, r_, ALU.mult)
            for cb in range(4):
                base = 4 * (cb % 2)
                for kb in range(4):
                    wv = wnext("dn")
                    for i in range(4):
                        for kc in range(8):
                            B.mm(bank(base + i), uT[:, kb * 8 + kc, i * 128:(i + 1) * 128], wv[:, kc, :],
                                 kb == 0 and kc == 0, kb == 3 and kc == 7)
                for i in range(4):
                    xr = xres[:, i, cb * 512:(cb + 1) * 512]
                    B.tt("dve", xr, xr, bank(base + i), ALU.add)
        if debug and debug[0] == "x2" and t == debug[2]:
            for i in range(4):
                dump(xres[:, i, :], i * 128, 0)
            break

        xs = cv(A, [128, 4, D], BF16)
        norm_to_hT([xres[:, i, :] for i in range(4)], gple, xs)
        pld = cv(A + 16384, [128, 4, 256], F32)
        pbf = cv(A + 20480, [128, 4, 256], BF16)
        pT = cv(A + 22528, [128, 2, T], BF16)
        ppt = [cv(A + 2048 * i, [128, 512], F32) for i in range(4)]
        tgt = [cv(A + 24576 + 2048 * i, [128, 512], F32) for i in range(4)]
        B.dma("sp", "pl", pld, p[tok0:tok0 + T, :].rearrange("(i p) c -> p i c", p=128))
        B.copy("dve", pbf, pld)
        for kc in range(2):
            pb_ = 6 + kc
            for i in range(4):
                B.tp(bank_bf(pb_)[:, i * 128:(i + 1) * 128], pbf[:, i, kc * 128:(kc + 1) * 128], ident_b)
            B.evac(pT[:, kc, :], bank_bf(pb_)[:, 0:512])
        for cb in range(4):
            base = 4 * (cb % 2)
            wv = wnext("pp")
            for i in range(4):
                for kc in range(2):
                    B.mm(bank(base + i), pT[:, kc, i * 128:(i + 1) * 128], wv[:, kc, :], kc == 0, kc == 1)
            for i in range(4):
                B.copy("act", ppt[i], bank(base + i))
            for kh in range(2):
                wv = wnext("pg")
                for i in range(4):
                    for kc in range(8):
                        B.mm(bank(base + i), hT[:, kh * 8 + kc, i * 128:(i + 1) * 128], wv[:, kc, :],
                             kh == 0 and kc == 0, kh == 1 and kc == 7)
            for i in range(4):
                B.act(tgt[i], bank(base + i), AF.Tanh, scale=0.5)
                B.stt("dve", tgt[i], tgt[i], 1.0, ppt[i], ALU.add, ALU.mult)
                xr = xres[:, i, cb * 512:(cb + 1) * 512]
                B.stt("dve", xr, tgt[i], 0.5, xr, ALU.mult, ALU.add)

        gfin = cv(A, [128, D], F32)
        ost = [cv(A + 8192 + 8192 * i, [128, D], F32) for i in range(2)]
        B.dma("sp", "gf", gfin, g_final[0:1, :].partition_broadcast(128))
        B.memset("dve", ssn[:, 0:4], 0.0)
        for i in range(4):
            B.act(ost[i % 2], xres[:, i, :], AF.Square, accum=ssn[:, i:i + 1])
        rstd(ssn[:, 4:8], ssn[:, 0:4], 1.0 / D, sc1[:, 0:4])
        for i in range(4):
            B.stt("dve", ost[i % 2], xres[:, i, :], ssn[:, 4 + i:5 + i], gfin, ALU.mult, ALU.mult)
            B.dma("sp", f"o{i % 2}", out[tok0 + i * 128:tok0 + (i + 1) * 128, :], ost[i % 2])

    finals = [s_ for s_ in B.dma_sems if s_ in ("o0", "o1") or s_.startswith("dbg")]
    B.emit(finals)
    return B


_INPUT_ORDER = ["x", "p", "g_mix", "w_in", "conv_w", "conv_b", "b_i", "b_f", "lam_q1", "lam_k1", "lam_q2", "lam_k2",
                "da_sub_g", "ml_norm_g", "w_pa", "w_pb", "w_o", "g_mlp", "w_up", "w_down", "g_ple", "w_ple_gate",
                "w_ple_proj", "g_final"]


def make_in_maps(inputs, n=8):
    f = lambda a: np.ascontiguousarray(np.asarray(a, dtype=np.float32))
    shared = {}
    for k in _INPUT_ORDER:
        if k in ("x", "p"):
            continue
        a = f(inputs[k])
        if k == "g_final":
            a = a.reshape(1, D)
        else:
            a = a[0]
            if a.ndim == 1:
                a = a.reshape(1, -1)
        shared[k] = a
    xs_ = f(inputs["x"])
    ps_ = f(inputs["p"])
    maps = []
    for c in range(n):
        m = dict(shared)
        m["x"] = xs_[c]
        m["p"] = ps_[0, c]
        maps.append(m)
    return maps


def kernel(**inputs):
    B = build()
    in_maps = make_in_maps(inputs, 8)
    res = run_bass_kernel_spmd(B.nc, in_maps, core_ids=list(range(8)))
    return np.stack([np.asarray(r["out"], dtype=np.float32) for r in res.results], axis=0)
```
